# Optimizing a Trainium2 kernel written in Bass

```python
import math
import jax, jax.numpy as jnp
from jax import lax
import numpy as np

D_MODEL = 2048
BATCH = 4
SEQ = 2048
DEPTH = 4
DEC_BATCH = 128
DEC_SEQ = 1
PAST_LEN = 16384
PAGE_SIZE = 128

N_MIXERS = 2
N_CONV_LAYERS = (DEPTH + 1) // N_MIXERS
N_HGRN_LAYERS = DEPTH // N_MIXERS
D_CONV = D_MODEL
CONV_WIDTH = 31
HGRN_EXPAND = 128
HGRN_HEADS = D_MODEL // HGRN_EXPAND
HGRN_DK = HGRN_EXPAND
HGRN_DV = D_MODEL // HGRN_HEADS
CHUNK = 64
D_FF = 5632
FFN_WIDTH = 3
EPS = 1e-6

kernel_name = "hybrid_conformer_hgrn2_convffn_step"


def rmsnorm(x, w):
    xf = x.astype(jnp.float32)
    y = xf * lax.rsqrt(jnp.mean(xf * xf, axis=-1, keepdims=True) + EPS)
    return (y * w.astype(jnp.float32)).astype(x.dtype)


def layernorm(x, w, b):
    xf = x.astype(jnp.float32)
    mu = jnp.mean(xf, axis=-1, keepdims=True)
    var = jnp.mean(jnp.square(xf - mu), axis=-1, keepdims=True)
    y = (xf - mu) * lax.rsqrt(var + EPS)
    return (y * w.astype(jnp.float32) + b.astype(jnp.float32)).astype(x.dtype)


def causal_depthwise_conv(x, buf, w, b):
    xx = jnp.concatenate([buf.astype(x.dtype), x], axis=1)
    c = x.shape[-1]
    y = lax.conv_general_dilated(xx, w.astype(x.dtype)[:, None, :], window_strides=(1,), padding='VALID',
                                 dimension_numbers=('NWC', 'WIO', 'NWC'), feature_group_count=c)
    return y + b.astype(x.dtype), xx[:, -buf.shape[1]:]


def conformer_conv(x, buf, w_pw1, b_pw1, w_dw, b_dw, ln_g, ln_b, w_pw2, b_pw2):
    h = x @ w_pw1 + b_pw1
    a, gt = jnp.split(h, 2, axis=-1)
    u = a * jax.nn.sigmoid(gt)
    c, new_buf = causal_depthwise_conv(u, buf, w_dw, b_dw)
    c = jax.nn.silu(layernorm(c, ln_g, ln_b))
    return c @ w_pw2 + b_pw2, new_buf


def gla_chunked(q, k, v, log_f, S0):
    B, L, H, DK = q.shape
    DV = v.shape[-1]
    C = math.gcd(L, CHUNK)
    NC = L // C

    def to_chunks(t):
        return t.astype(jnp.float32).reshape(B, NC, C, H, t.shape[-1]).transpose(1, 0, 2, 3, 4)

    mask = jnp.tril(jnp.ones((C, C), dtype=bool))

    def step(S, inp):
        qc, kc, vc, gc = inp
        G = jnp.cumsum(gc, axis=1)
        o_inter = jnp.einsum('bthk,bhkv->bthv', qc * jnp.exp(G), S)
        rel = jnp.where(mask[None, :, :, None, None], G[:, :, None] - G[:, None, :], -jnp.inf)
        decay = jnp.exp(rel)
        A = jnp.einsum('bthk,btshk,bshk->bths', qc, decay, kc)
        o_intra = jnp.einsum('bths,bshv->bthv', A, vc)
        G_last = G[:, -1]
        S_new = jnp.exp(G_last)[..., None] * S + jnp.einsum(
            'bshk,bshv->bhkv', kc * jnp.exp(G_last[:, None] - G), vc)
        return S_new, o_inter + o_intra

    S, o = lax.scan(step, S0.astype(jnp.float32), (to_chunks(q), to_chunks(k), to_chunks(v), to_chunks(log_f)))
    o = o.transpose(1, 0, 2, 3, 4).reshape(B, L, H, DV)
    return o, S


def hgrn2_mixer(x, S0, w_q, w_f, w_i, w_g, w_o, lb, norm_g):
    B, L, _ = x.shape
    q = jax.nn.silu(x @ w_q).reshape(B, L, HGRN_HEADS, HGRN_DK)
    z = (x @ w_f).astype(jnp.float32).reshape(B, L, HGRN_HEADS, HGRN_DK)
    lb = lb.reshape(HGRN_HEADS, HGRN_DK)
    log_f = jnp.logaddexp(jnp.log(lb), jnp.log1p(-lb) + jax.nn.log_sigmoid(z))
    k = (1.0 - lb) * jax.nn.sigmoid(-z)
    v = (x @ w_i).reshape(B, L, HGRN_HEADS, HGRN_DV)
    o, S = gla_chunked(q, k, v, log_f, S0)
    o = rmsnorm(o.astype(x.dtype), norm_g.reshape(HGRN_HEADS, HGRN_DV))
    o = o.reshape(B, L, D_MODEL) * jax.nn.silu(x @ w_g)
    return o @ w_o, S.astype(S0.dtype)


def conv_ffn(x, buf, w_up, w_dw, b_dw, w_down):
    h = x @ w_up
    h, new_buf = causal_depthwise_conv(h, buf, w_dw, b_dw)
    gate, up = jnp.split(h, 2, axis=-1)
    return (jax.nn.silu(gate) * up) @ w_down, new_buf


def trunk(x, conv_state, hgrn_state, ffn_state, p):
    (norm_mix, norm_ffn, norm_final,
     conv_w_pw1, conv_b_pw1, conv_w_dw, conv_b_dw, conv_ln_g, conv_ln_b, conv_w_pw2, conv_b_pw2,
     hgrn_w_q, hgrn_w_f, hgrn_w_i, hgrn_w_g, hgrn_w_o, hgrn_lb, hgrn_norm_g,
     ffn_w_up, ffn_w_dw, ffn_b_dw, ffn_w_down) = p
    new_conv, new_hgrn, new_ffn = [], [], []
    for i in range(DEPTH):
        j = i // N_MIXERS
        h = rmsnorm(x, norm_mix[i])
        if i % N_MIXERS == 0:
            y, s = conformer_conv(h, conv_state[j], conv_w_pw1[j], conv_b_pw1[j], conv_w_dw[j], conv_b_dw[j],
                                  conv_ln_g[j], conv_ln_b[j], conv_w_pw2[j], conv_b_pw2[j])
            new_conv.append(s)
        else:
            y, s = hgrn2_mixer(h, hgrn_state[j], hgrn_w_q[j], hgrn_w_f[j], hgrn_w_i[j], hgrn_w_g[j],
                               hgrn_w_o[j], hgrn_lb[j], hgrn_norm_g[j])
            new_hgrn.append(s)
        x = x + y
        h = rmsnorm(x, norm_ffn[i])
        y, s = conv_ffn(h, ffn_state[i], ffn_w_up[i], ffn_w_dw[i], ffn_b_dw[i], ffn_w_down[i])
        new_ffn.append(s)
        x = x + y
    return rmsnorm(x, norm_final), jnp.stack(new_conv), jnp.stack(new_hgrn), jnp.stack(new_ffn)


def setup_inputs(seed: int = 0) -> dict:
    key = jax.random.key(seed)
    ks = jax.random.split(key, 32)

    def nrm(k, shape, scale):
        return jax.random.normal(k, shape, dtype=jnp.float32) * scale

    D = D_MODEL
    return {
        'x_prompt': nrm(ks[0], (BATCH, SEQ, D), 1.0),
        'x_sample': nrm(ks[1], (DEC_BATCH, DEC_SEQ, D), 1.0),
        'state_conv': nrm(ks[2], (N_CONV_LAYERS, DEC_BATCH, CONV_WIDTH - 1, D_CONV), 0.5),
        'state_hgrn': nrm(ks[3], (N_HGRN_LAYERS, DEC_BATCH, HGRN_HEADS, HGRN_DK, HGRN_DV), 0.3),
        'state_ffn': nrm(ks[4], (DEPTH, DEC_BATCH, FFN_WIDTH - 1, 2 * D_FF), 1.0),
        'norm_mix': 1.0 + nrm(ks[5], (DEPTH, D), 0.02),
        'norm_ffn': 1.0 + nrm(ks[6], (DEPTH, D), 0.02),
        'norm_final': 1.0 + nrm(ks[7], (D,), 0.02),
        'conv_w_pw1': nrm(ks[8], (N_CONV_LAYERS, D, 2 * D_CONV), D ** -0.5),
        'conv_b_pw1': nrm(ks[9], (N_CONV_LAYERS, 2 * D_CONV), 0.02),
        'conv_w_dw': nrm(ks[10], (N_CONV_LAYERS, CONV_WIDTH, D_CONV), CONV_WIDTH ** -0.5),
        'conv_b_dw': nrm(ks[11], (N_CONV_LAYERS, D_CONV), 0.02),
        'conv_ln_g': 1.0 + nrm(ks[12], (N_CONV_LAYERS, D_CONV), 0.02),
        'conv_ln_b': nrm(ks[13], (N_CONV_LAYERS, D_CONV), 0.02),
        'conv_w_pw2': nrm(ks[14], (N_CONV_LAYERS, D_CONV, D), D_CONV ** -0.5),
        'conv_b_pw2': nrm(ks[15], (N_CONV_LAYERS, D), 0.02),
        'hgrn_w_q': nrm(ks[16], (N_HGRN_LAYERS, D, HGRN_HEADS * HGRN_DK), D ** -0.5),
        'hgrn_w_f': nrm(ks[17], (N_HGRN_LAYERS, D, HGRN_HEADS * HGRN_DK), D ** -0.5),
        'hgrn_w_i': nrm(ks[18], (N_HGRN_LAYERS, D, HGRN_HEADS * HGRN_DV), D ** -0.5),
        'hgrn_w_g': nrm(ks[19], (N_HGRN_LAYERS, D, D), D ** -0.5),
        'hgrn_w_o': nrm(ks[20], (N_HGRN_LAYERS, D, D), D ** -0.5),
        'hgrn_lb_raw': nrm(ks[21], (N_HGRN_LAYERS, HGRN_HEADS * HGRN_DK), 1.0),
        'hgrn_norm_g': 1.0 + nrm(ks[22], (N_HGRN_LAYERS, D), 0.02),
        'ffn_w_up': nrm(ks[23], (DEPTH, D, 2 * D_FF), D ** -0.5),
        'ffn_w_dw': nrm(ks[24], (DEPTH, FFN_WIDTH, 2 * D_FF), FFN_WIDTH ** -0.5),
        'ffn_b_dw': nrm(ks[25], (DEPTH, 2 * D_FF), 0.02),
        'ffn_w_down': nrm(ks[26], (DEPTH, D_FF, D), D_FF ** -0.5),
    }


def reference(x_prompt, x_sample, state_conv, state_hgrn, state_ffn,
              norm_mix, norm_ffn, norm_final,
              conv_w_pw1, conv_b_pw1, conv_w_dw, conv_b_dw, conv_ln_g, conv_ln_b, conv_w_pw2, conv_b_pw2,
              hgrn_w_q, hgrn_w_f, hgrn_w_i, hgrn_w_g, hgrn_w_o, hgrn_lb_raw, hgrn_norm_g,
              ffn_w_up, ffn_w_dw, ffn_b_dw, ffn_w_down):
    lb_cum = jnp.cumsum(jax.nn.softmax(hgrn_lb_raw.astype(jnp.float32), axis=0), axis=0)
    hgrn_lb = lb_cum - lb_cum[0:1]
    params = (norm_mix, norm_ffn, norm_final,
              conv_w_pw1, conv_b_pw1, conv_w_dw, conv_b_dw, conv_ln_g, conv_ln_b, conv_w_pw2, conv_b_pw2,
              hgrn_w_q, hgrn_w_f, hgrn_w_i, hgrn_w_g, hgrn_w_o, hgrn_lb, hgrn_norm_g,
              ffn_w_up, ffn_w_dw, ffn_b_dw, ffn_w_down)
    dt = x_prompt.dtype
    zero_conv = jnp.zeros((N_CONV_LAYERS, BATCH, CONV_WIDTH - 1, D_CONV), dtype=dt)
    zero_hgrn = jnp.zeros((N_HGRN_LAYERS, BATCH, HGRN_HEADS, HGRN_DK, HGRN_DV), dtype=dt)
    zero_ffn = jnp.zeros((DEPTH, BATCH, FFN_WIDTH - 1, 2 * D_FF), dtype=dt)
    y_prompt, conv_p, hgrn_p, ffn_p = trunk(x_prompt, zero_conv, zero_hgrn, zero_ffn, params)
    y_sample, conv_s, hgrn_s, ffn_s = trunk(x_sample, state_conv, state_hgrn, state_ffn, params)
    return (y_prompt, y_sample, conv_p, hgrn_p, ffn_p, conv_s, hgrn_s, ffn_s)
```

```python
import numpy as np
import concourse.bass as bass
import concourse.mybir as mybir
from concourse.bass_utils import run_bass_kernel_spmd

F32 = mybir.dt.float32
BF16 = mybir.dt.bfloat16
AF = mybir.ActivationFunctionType
ALU = mybir.AluOpType
AX = mybir.AxisListType
P = 128
CWID = 31
EPS = 1e-6
GC = 32


class Cfg:
    def __init__(self, D=2048, FF=5632, SEQ=2048, T=512, NS=16, DEPTH=4):
        self.D, self.FF, self.SEQ, self.T, self.NS, self.DEPTH = D, FF, SEQ, T, NS, DEPTH
        self.KD = D // P
        self.KF = FF // P
        self.H = self.KD
        self.NTL = SEQ // T
        self.NCL = (DEPTH + 1) // 2
        self.NHL = DEPTH // 2
        self.CW = min(512, D)
        self.MG = self.CW // P
        kg = 1
        for d in range(1, 17):
            if self.KF % d == 0:
                kg = d
        self.KG = kg
        self.WK = max(self.KD, self.KG)


class PV:
    def __init__(self):
        self.cols = []
        self.off = {}
        self.n = 0

    def add(self, name, arr2d):
        self.off[name] = self.n
        self.cols.append(np.ascontiguousarray(arr2d, dtype=np.float32))
        self.n += arr2d.shape[1]

    def vec(self, name, v):
        self.add(name, np.asarray(v).reshape(-1, P).T)

    def taps(self, name, w):
        nt = w.shape[0]
        a = np.asarray(w).reshape(nt, -1, P).transpose(2, 1, 0)
        self.add(name, a.reshape(P, -1))

    def build(self):
        return np.ascontiguousarray(np.concatenate(self.cols, axis=1))


def pack_params(cfg, inp, core=0):
    pv = PV()
    D, FF = cfg.D, cfg.FF
    isB = core % 2
    gl = [2 * isB, 2 * isB + 1]
    gj = isB
    for ll, l in enumerate(gl):
        pv.vec(f"Pnm{ll}", inp["norm_mix"][l])
        pv.vec(f"Pnf{ll}", inp["norm_ffn"][l])
        pv.taps(f"Pfw{ll}", inp["ffn_w_dw"][l])
        pv.vec(f"Pfb{ll}", inp["ffn_b_dw"][l])
    pv.vec("Pcb1a0", inp["conv_b_pw1"][gj][:D])
    pv.vec("Pcb1g0", inp["conv_b_pw1"][gj][D:])
    pv.taps("Pcw0", inp["conv_w_dw"][gj])
    pv.vec("Pcbd0", inp["conv_b_dw"][gj])
    pv.vec("Pclg0", inp["conv_ln_g"][gj])
    pv.vec("Pclb0", inp["conv_ln_b"][gj])
    pv.vec("Pcb20", inp["conv_b_pw2"][gj])
    pv.vec("Phng0", inp["hgrn_norm_g"][gj])
    pv.vec("Pnfin", inp["norm_final"])
    for nm in ("mA", "PmA"):
        pv.add(nm, np.full((P, 1), 1.0 - isB))
    for nm in ("mB", "PmB"):
        pv.add(nm, np.full((P, 1), float(isB)))
    for l in range(cfg.DEPTH):
        pv.vec(f"nm{l}", inp["norm_mix"][l])
        pv.vec(f"nf{l}", inp["norm_ffn"][l])
        pv.taps(f"fw{l}", inp["ffn_w_dw"][l])
        pv.vec(f"fb{l}", inp["ffn_b_dw"][l])
    for j in range(cfg.NCL):
        pv.vec(f"cb1a{j}", inp["conv_b_pw1"][j][:D])
        pv.vec(f"cb1g{j}", inp["conv_b_pw1"][j][D:])
        pv.taps(f"cw{j}", inp["conv_w_dw"][j])
        pv.vec(f"cbd{j}", inp["conv_b_dw"][j])
        pv.vec(f"clg{j}", inp["conv_ln_g"][j])
        pv.vec(f"clb{j}", inp["conv_ln_b"][j])
        pv.vec(f"cb2{j}", inp["conv_b_pw2"][j])
    for j in range(cfg.NHL):
        pv.vec(f"hraw{j}", inp["hgrn_lb_raw"][j])
        pv.vec(f"hng{j}", inp["hgrn_norm_g"][j])
    pv.vec("nfin", inp["norm_final"])
    return pv


def make_consts(cfg):
    c = {}
    cols = []
    n = 0

    def add(name, a):
        nonlocal n
        c[name] = n
        cols.append(a.astype(np.float32))
        n += a.shape[1]

    add("ident", np.eye(P))
    tri = np.zeros((P, GC))
    tri[:GC] = np.triu(np.ones((GC, GC)))
    add("triu", tri)
    rm = np.ones((P, 512))
    rm[:, ::GC] = 0.0
    add("rmask", rm)
    add("ones", np.ones((P, 64)))
    add("eps", np.full((P, 1), EPS))
    add("one", np.ones((P, 1)))
    return c, np.ascontiguousarray(np.concatenate(cols, axis=1)).astype(np.float32)


class Buf:
    __slots__ = ("name", "w", "r")

    def __init__(self, name):
        self.name = name
        self.w = None
        self.r = []


class Eng:
    def __init__(self, name, obj):
        self.name, self.obj = name, obj
        self.sem = None
        self.count = 0
        self.seen = {}


class Builder:
    def __init__(self, cfg, pvoff, npv, coff, ncst):
        self.cfg = cfg
        self.pvoff, self.npv, self.coff, self.ncst = pvoff, npv, coff, ncst
        self.nc = nc = bass.Bass("TRN2", target_bir_lowering=False)
        self.eng = {
            "pe": Eng("pe", nc.tensor), "act": Eng("act", nc.scalar), "dve": Eng("dve", nc.vector),
            "pool": Eng("pool", nc.gpsimd), "sp": Eng("sp", nc.sync),
        }
        self.sems = []
        self.dsems = {}
        self.planning = False
        self.pfx = ""
        self.plan = []
        self.nbuf = 0
        self.phase_bufs = []
        self.prev_phase_tokens = []

    def newsem(self, name):
        h = self.nc.alloc_semaphore(name)
        self.sems.append(h)
        return len(self.sems) - 1

    def buf(self, name="b"):
        self.nbuf += 1
        return Buf(f"{name}{self.nbuf}")

    def bufs(self, n, name="b"):
        return [self.buf(name) for _ in range(n)]

    def wbuf(self, name="w"):
        b = self.buf(name)
        self.phase_bufs.append(b)
        return b

    def _waits(self, e, r, w):
        need = {}

        def add(tok):
            if tok is None:
                return
            k, v = tok
            if need.get(k, 0) < v:
                need[k] = v

        for b in r:
            add(b.w)
        for b in w:
            add(b.w)
            for t in b.r:
                add(t)
        for k, v in need.items():
            if e.name == "pe" and k == e.sem:
                continue
            if e.seen.get(k, 0) < v:
                e.obj.wait_ge(self.sems[k], v)
                e.seen[k] = v

    def _commit(self, tok, r, w):
        for b in r:
            b.r.append(tok)
            if len(b.r) > 24:
                m = {}
                for k, v in b.r:
                    if m.get(k, 0) < v:
                        m[k] = v
                b.r = list(m.items())
        for b in w:
            b.w = tok
            b.r = []

    def op(self, E, fn, r=(), w=()):
        if self.planning:
            return None
        e = self.eng[E]
        if e.sem is None or e.count >= 30000:
            e.sem = self.newsem(f"s_{e.name}_{len(self.sems)}")
            e.count = 0
        self._waits(e, r, w)
        ins = fn(e.obj)
        e.count += 1
        ins.then_inc(self.sems[e.sem], 1)
        tok = (e.sem, e.count)
        self._commit(tok, r, w)
        return tok

    def dma(self, E, out, in_, r=(), w=(), key=None):
        if self.planning:
            return None
        e = self.eng[E]
        self._waits(e, r, w)
        if key not in self.dsems:
            self.dsems[key] = [self.newsem(f"d_{len(self.sems)}"), 0]
        d = self.dsems[key]
        ins = e.obj.dma_start(out=out, in_=in_)
        d[1] += 16
        ins.then_inc(self.sems[d[0]], 16)
        tok = (d[0], d[1])
        self._commit(tok, r, w)
        return tok

    def phase(self):
        toks = []
        for b in self.phase_bufs:
            if b.w is not None:
                toks.append(b.w)
            toks.extend(b.r)
        m = {}
        for k, v in toks:
            if m.get(k, 0) < v:
                m[k] = v
        self.prev_phase_tokens = list(m.items())
        self.phase_bufs = []

    def wbufp(self, name="w"):
        b = self.wbuf(name)
        b.r = list(self.prev_phase_tokens)
        return b

    def act(self, out, in_, func, r, w, bias=None, scale=None):
        kw = {}
        if bias is not None:
            kw["bias"] = bias
        if scale is not None:
            kw["scale"] = scale
        return self.op("act", lambda e: e.activation(out, in_, func, **kw), r, w)

    def ts(self, out, in0, s1, s2, op0, op1, r, w, E="dve"):
        return self.op(E, lambda e: e.tensor_scalar(out, in0, s1, s2, op0, op1), r, w)

    def stt(self, out, in0, sc, in1, op0, op1, r, w):
        return self.op("dve", lambda e: e.scalar_tensor_tensor(out, in0, sc, in1, op0, op1), r, w)

    def tt(self, out, in0, in1, op, r, w, E="dve"):
        return self.op(E, lambda e: e.tensor_tensor(out, in0, in1, op), r, w)

    def cp(self, E, out, in_, r, w):
        if E == "act":
            return self.op("act", lambda e: e.copy(out, in_), r, w)
        return self.op(E, lambda e: e.tensor_copy(out, in_), r, w)

    def mm(self, out, pairs, r, w, transpose=False):
        n = len(pairs)

        def fn(e):
            ins = None
            for i, (a, b) in enumerate(pairs):
                ins = e.matmul(out, a, b, start=(i == 0), stop=(i == n - 1))
            return ins

        return self.op("pe", fn, r, w)

    def tr(self, out, in_, ident, r, w):
        return self.op("pe", lambda e: e.transpose(out, in_, ident), r, w)

    def pvc(self, name, k=0, n=1):
        o = self.pvoff[self.pfx + name] + k
        return self.pv[:, o:o + n]

    def coll_allgather(self, in_ap, out_ap, r, w):
        if self.planning:
            return None
        e = self.eng["pool"]
        self._waits(e, r, w)
        if "cc" not in self.dsems:
            self.dsems["cc"] = [self.newsem("cc"), 0]
        d = self.dsems["cc"]
        ins = e.obj.collective_compute("AllGather", ALU.bypass, replica_groups=[[0, 1], [2, 3], [4, 5], [6, 7]],
                                       ins=[in_ap], outs=[out_ap])
        d[1] += 1
        ins.then_inc(self.sems[d[0]], 1)
        tok = (d[0], d[1])
        self._commit(tok, r, w)
        return tok

    def set_mode(self, mode, oslot=0):
        if mode == "P":
            self.W, self.pfx = self.WP, "P"
            self.lbv, self.omlv = self.lbvP, self.omlvP
            self.O["nconv_p"] = self.Oall["nconv_p"][oslot:oslot + 1]
            self.O["nhgrn_p"] = self.Oall["nhgrn_p"][oslot:oslot + 1]
            self.O["nffn_p"] = self.Oall["nffn_p"][oslot]
        else:
            self.W, self.pfx = self.WS, ""
            self.lbv, self.omlv = self.lbvS, self.omlvS

    def wnext(self, *pieces):
        if self.planning:
            self.plan.append(pieces)
            return self.wslot[0], self.wslotb[0]
        i = self.wi
        NSL = len(self.wslot)
        while self.wissued < min(i + NSL, len(self.plan)):
            j = self.wissued
            s = j % NSL
            c0 = 0
            for (v, k2, c2) in self.plan[j]:
                self.dma("pool", self.wslot[s][:, :k2, c0:c0 + c2], v, r=(), w=(self.wslotb[s],), key=("w", s))
                c0 += c2
            self.wissued += 1
        self.wi += 1
        return self.wslot[i % NSL], self.wslotb[i % NSL]

    def wcols(self, W, l, c0, cols):
        K = W.shape[1]
        return W[l, :, c0:c0 + cols].rearrange("(kc p) n -> p kc n", p=P), K // P, cols

    def alloc(self):
        cfg, nc = self.cfg, self.nc
        KD, KF, T, NS, D, FF = cfg.KD, cfg.KF, cfg.T, cfg.NS, cfg.D, cfg.FF
        A = nc.alloc_sbuf_tensor
        self.pv = A("pv_sb", [P, self.npv], F32)
        self.pvb = self.buf("pv")
        self.cst = A("cst_sb", [P, self.ncst], F32)
        self.cstb = self.buf("cst")
        self.identb = A("identb", [P, P], BF16)
        self.onesb = A("onesb", [P, P], BF16)
        self.lbvS = A("lbvS", [P, cfg.NHL, KD], F32)
        self.omlvS = A("omlvS", [P, cfg.NHL, KD], F32)
        self.lbvP = A("lbvP", [P, 1, KD], F32)
        self.omlvP = A("omlvP", [P, 1, KD], F32)
        self.lbv, self.omlv = self.lbvS, self.omlvS
        self.lbb = self.buf("lb")
        self.x = A("x_sb", [P, KD, T], F32)
        self.xb = self.bufs(KD, "x")
        self.h = A("h_sb", [P, KD, T], BF16)
        self.hb = self.bufs(KD, "h")
        self.wslot = [A(f"ws{i}", [P, cfg.WK, 512], BF16) for i in range(3)]
        self.wslotb = self.bufs(3, "ws")
        self.uh = [A(f"uh{j}", [P, KD, CWID - 1], BF16) for j in range(1)]
        self.uhb = self.bufs(1, "uh")
        self.Sf = [A(f"Sf{j}", [P, KD, P], F32) for j in range(1)]
        self.Sb = [A(f"Sb{j}", [P, KD, P], BF16) for j in range(1)]
        self.Sfb = [self.bufs(KD, "Sf") for j in range(1)]
        self.Sbb = [self.bufs(KD, "Sb") for j in range(1)]
        self.fh = [A(f"fh{l}", [P, 2 * KF, 2], F32) for l in range(2)]
        self.fhb = [self.bufs(2 * KF, "fh") for l in range(2)]
        self.NQ = min(4, KD)
        self.xob = self.bufs(self.NQ, "xo")
        self.xgb = [self.bufs(self.NQ, "xg") for _ in range(2)]
        self.WB = 66 * 1024
        self.work = A("work", [P, self.WB // 2], BF16)
        self.woff = 0
        self.ps = [nc.alloc_psum_tensor(f"ps{i}", [P, 512], F32) for i in range(8)]
        self.psb = self.bufs(8, "ps")

    def wreset(self):
        self.woff = 0

    def wt(self, shape, dt):
        n = int(np.prod(shape[1:]))
        nb = n * (4 if dt == F32 else 2)
        nb = (nb + 31) // 32 * 32
        assert self.woff + nb <= self.WB, (self.woff, nb, self.WB)
        a = self.work[:, self.woff // 2:(self.woff + nb) // 2]
        self.woff += nb
        if dt == F32:
            a = a.bitcast(F32)
        a = a[:, :n]
        if len(shape) == 3:
            a = a.rearrange("p (a b) -> p a b", a=shape[1])
        elif len(shape) == 4:
            a = a.rearrange("p (a b c) -> p a b c", a=shape[1], b=shape[2])
        return a

    def c(self, name, n=1, k=0):
        o = self.coff[name] + k
        return self.cst[:, o:o + n]

    def rstd_from(self, ps_ap, psbuf, n, scale, out, outb, tmp, tmpb):
        self.act(tmp, ps_ap, AF.Sqrt, r=(psbuf, self.cstb), w=(tmpb,), bias=self.c("eps"), scale=scale)
        self.op("dve", lambda e: e.reciprocal(out, tmp), r=(tmpb,), w=(outb,))

    def rmsnorm(self, tcx, wname, dst, dstb, scr):
        cfg = self.cfg
        KD, N = cfg.KD, tcx["N"]
        x, xb = tcx["x"], tcx["xb"]
        sq = [self.wt([P, N], BF16) for _ in range(2)]
        sqb = [self.wbufp("sq") for _ in range(2)]
        (rs, rsb), (tmp, tmpb) = scr
        pss, pssb = self.ps[4], self.psb[4]
        toks = []
        for kc in range(KD):
            s = kc % 2
            self.act(sq[s][:, :N], x[:, kc, :N], AF.Square, r=(xb[kc],), w=(sqb[s],))
            self.op("pe", (lambda e, s=s, kc=kc: e.matmul(pss[:, :N], self.onesb[:, :], sq[s][:, :N],
                                                       start=(kc == 0), stop=(kc == KD - 1))),
                    r=(sqb[s], self.cstb), w=(pssb,))
        self.rstd_from(pss[:, :N], pssb, N, 1.0 / cfg.D, rs[:, :N], rsb, tmp[:, :N], tmpb)
        for kc in range(KD):
            self.stt(dst[:, kc, :N], x[:, kc, :N], self.pvc(wname, kc), rs[:, :N], ALU.mult, ALU.mult,
                     r=(xb[kc], rsb, self.pvb), w=(dstb[kc],))

    def proj(self, ps_i, wt_, wtb, col0, KC, rhs, rhsb, N):
        pairs = [(wt_[:, kc, col0:col0 + P], rhs[:, kc, :N]) for kc in range(KC)]
        return self.mm(self.ps[ps_i][:, :N], pairs, r=(wtb,) + tuple(rhsb[:KC]), w=(self.psb[ps_i],))

    def load_fm(self, dst, dstb, src_rows, n, nchunks, stage, stageb, key, dst_col0=0):
        self.dma("sp", stage[:n, :nchunks * P], src_rows, r=(), w=(stageb,), key=key)
        for k0 in range(0, nchunks, 4):
            kn = min(4, nchunks - k0)
            pi = 6 + (k0 // 4) % 2
            for k in range(kn):
                self.tr(self.ps[pi][:, k * P:k * P + n], stage[:n, (k0 + k) * P:(k0 + k + 1) * P],
                        self.c("ident", n)[:n, :], r=(stageb, self.cstb), w=(self.psb[pi],))
            src = self.ps[pi][:, :kn * P].rearrange("p (k t) -> p k t", k=kn)[:, :, :n]
            self.cp("act" if (k0 // 4) % 2 == 0 else "dve", dst[:, k0:k0 + kn, dst_col0:dst_col0 + n], src,
                    r=(self.psb[pi],), w=tuple(dstb[k0:k0 + kn]))

    def store_tm(self, dst_rows, src, srcb, n, nchunks, stage, stageb, key, src_col0=0):
        for k0 in range(0, nchunks, 4):
            kn = min(4, nchunks - k0)
            q = (k0 // 4) % 2
            pi = 6 + q
            for k in range(kn):
                self.tr(self.ps[pi][:n, k * P:(k + 1) * P], src[:, k0 + k, src_col0:src_col0 + n],
                        self.c("ident", P), r=(srcb[k0 + k], self.cstb), w=(self.psb[pi],))
            self.cp("act" if q == 0 else "dve", stage[q][:n, :kn * P],
                    self.ps[pi][:n, :kn * P], r=(self.psb[pi],), w=(stageb[q],))
            self.dma("sp", dst_rows[:, k0 * P:(k0 + kn) * P], stage[q][:n, :kn * P], r=(stageb[q],), w=(),
                     key=(key, q))

    def conv_mixer(self, tcx, l):
        cfg = self.cfg
        j = l // 2
        KD, N, D, MG, CW = cfg.KD, tcx["N"], cfg.D, cfg.MG, cfg.CW
        x, xb = tcx["x"], tcx["xb"]
        samp = tcx["kind"] == "s"
        self.phase()
        self.wreset()
        HL = CWID - 1
        ub = self.wt([P, KD, HL + N], BF16)
        ubb = [self.wbufp("ub") for _ in range(KD)]
        cb, cbb = self.h, self.hb
        uf = self.wt([P, KD, HL if not samp else N], F32)
        ufb = [self.wbufp("uf") for _ in range(KD)]
        scr = [(self.wt([P, N], F32), self.wbufp("scr")) for _ in range(8)]
        sg = [scr[0][0], scr[1][0]]
        sgb = [scr[0][1], scr[1][1]]
        stq = [self.wt([P, 512], F32) for _ in range(2)]
        stqb = [self.wbufp("stq") for _ in range(2)]
        self.rmsnorm(tcx, f"nm{l}", self.h, self.hb, (scr[2], scr[3]))
        if not samp:
            for kc in range(KD):
                self.cp("act", ub[:, kc, 0:HL], self.uh[j][:, kc, :], r=(self.uhb[j],), w=(ubb[kc],))
        W1, W2 = self.W["conv_w_pw1"], self.W["conv_w_pw2"]
        CH = CW // 2
        for cg in range(D // CH):
            wa, wab = self.wnext(self.wcols(W1, j, cg * CH, CH), self.wcols(W1, j, D + cg * CH, CH))
            for m in range(CH // P):
                c = cg * (CH // P) + m
                pa, pg = m % 2, 2 + m % 2
                self.proj(pa, wa, wab, m * P, KD, self.h, self.hb, N)
                self.proj(pg, wa, wab, CH + m * P, KD, self.h, self.hb, N)
                s = m % 2
                self.act(sg[s][:, :N], self.ps[pg][:, :N], AF.Sigmoid, r=(self.psb[pg], self.pvb), w=(sgb[s],),
                         bias=self.pvc(f"cb1g{j}", c))
                if not samp:
                    self.stt(ub[:, c, HL:HL + N], self.ps[pa][:, :N], self.pvc(f"cb1a{j}", c), sg[s][:, :N],
                             ALU.add, ALU.mult, r=(self.psb[pa], sgb[s], self.pvb), w=(ubb[c],))
                    if tcx["last"]:
                        self.stt(uf[:, c, :], self.ps[pa][:, N - HL:N], self.pvc(f"cb1a{j}", c),
                                 sg[s][:, N - HL:N], ALU.add, ALU.mult,
                                 r=(self.psb[pa], sgb[s], self.pvb), w=(ufb[c],))
                else:
                    self.stt(uf[:, c, :N], self.ps[pa][:, :N], self.pvc(f"cb1a{j}", c), sg[s][:, :N],
                             ALU.add, ALU.mult, r=(self.psb[pa], sgb[s], self.pvb), w=(ufb[c],))
        sq = [self.wt([P, N], BF16) for _ in range(2)]
        sqb = [self.wbufp("sq") for _ in range(2)]
        ps1, ps2 = self.ps[4], self.ps[5]
        if not samp:
            dg = [self.wt([P, CWID, P], BF16) for _ in range(2)]
            dgb = [self.wbufp("dg") for _ in range(2)]
            for c in range(KD):
                s = c % 2
                self.tt(dg[s][:, :, :], self.identb[:, :].unsqueeze(1).to_broadcast([P, CWID, P]),
                        self.pvc(f"cw{j}", c * CWID, CWID).unsqueeze(2).to_broadcast([P, CWID, P]), ALU.mult,
                        r=(self.pvb, self.cstb), w=(dgb[s],), E=("dve" if c % 2 == 0 else "pool"))
                pc = c % 2
                pairs = [(dg[s][:, tp, :], ub[:, c, tp:tp + N]) for tp in range(CWID)]
                self.mm(self.ps[pc][:, :N], pairs, r=(dgb[s], ubb[c]), w=(self.psb[pc],))
                self.act(cb[:, c, :N], self.ps[pc][:, :N], AF.Identity, r=(self.psb[pc], self.pvb), w=(cbb[c],),
                         bias=self.pvc(f"cbd{j}", c))
                self.act(sq[s][:, :N], self.ps[pc][:, :N], AF.Square, r=(self.psb[pc], self.pvb), w=(sqb[s],),
                         bias=self.pvc(f"cbd{j}", c))
                self.op("pe", (lambda e, c=c: e.matmul(ps1[:, :N], self.onesb[:, :], cb[:, c, :N],
                                                     start=(c == 0), stop=(c == KD - 1))),
                        r=(cbb[c], self.cstb), w=(self.psb[4],))
                self.op("pe", (lambda e, c=c, s=s: e.matmul(ps2[:, :N], self.onesb[:, :], sq[s][:, :N],
                                                          start=(c == 0), stop=(c == KD - 1))),
                        r=(sqb[s], self.cstb), w=(self.psb[5],))
            for kc in range(KD):
                self.cp("dve", self.uh[j][:, kc, :], ub[:, kc, N:N + HL], r=(ubb[kc],), w=(self.uhb[j],))
            if tcx["last"]:
                self.store_tm(self.O["nconv_p"][j], uf, ufb, HL, KD, stq, stqb, key="nconvp")
        else:
            NSg = 4
            st = [self.wt([P, D], F32) for _ in range(2)]
            stb = [self.wbufp("st") for _ in range(2)]
            prod = [self.wt([P, NSg, HL], F32) for _ in range(2)]
            prodb = [self.wbufp("prod") for _ in range(2)]
            red = self.wt([P, KD, N], F32)
            redb = [self.wbufp("red") for _ in range(KD)]
            cs = [self.wt([P, 16], F32) for _ in range(2)]
            csb = [self.wbufp("cs") for _ in range(2)]
            src = self.I["st_conv"][j].rearrange("s j d -> (s j) d")
            R = NSg * HL
            for rg in range(N // NSg):
                s = rg % 2
                self.dma("sp", st[s][:R, :], src[rg * R:(rg + 1) * R, :], r=(), w=(stb[s],), key=("stc", s))
                for c in range(KD):
                    pi = 6 + c % 2
                    self.tr(self.ps[pi][:, :R], st[s][:R, c * P:(c + 1) * P], self.c("ident", R)[:R, :],
                            r=(stb[s], self.cstb), w=(self.psb[pi],))
                    q = c % 2
                    wv = self.pvc(f"cw{j}", c * CWID, HL).unsqueeze(1).to_broadcast([P, NSg, HL])
                    self.tt(prod[q][:, :, :], self.ps[pi][:, :R].rearrange("p (s j) -> p s j", s=NSg), wv, ALU.mult,
                            r=(self.psb[pi], self.pvb), w=(prodb[q],))
                    self.op("dve", (lambda e, q=q, c=c, rg=rg: e.tensor_reduce(
                        red[:, c, rg * NSg:(rg + 1) * NSg], prod[q][:, :, :], AX.X, ALU.add)),
                        r=(prodb[q],), w=(redb[c],))
            for c in range(KD):
                s = c % 2
                self.stt(cs[s][:, :N], uf[:, c, :N], self.pvc(f"cw{j}", c * CWID + HL), red[:, c, :N],
                         ALU.mult, ALU.add, r=(ufb[c], redb[c], self.pvb), w=(csb[s],))
                self.act(cb[:, c, :N], cs[s][:, :N], AF.Identity, r=(csb[s], self.pvb), w=(cbb[c],),
                         bias=self.pvc(f"cbd{j}", c))
                self.act(sq[s][:, :N], cs[s][:, :N], AF.Square, r=(csb[s], self.pvb), w=(sqb[s],),
                         bias=self.pvc(f"cbd{j}", c))
                self.op("pe", (lambda e, c=c: e.matmul(ps1[:, :N], self.onesb[:, :], cb[:, c, :N],
                                                     start=(c == 0), stop=(c == KD - 1))),
                        r=(cbb[c], self.cstb), w=(self.psb[4],))
                self.op("pe", (lambda e, c=c, s=s: e.matmul(ps2[:, :N], self.onesb[:, :], sq[s][:, :N],
                                                          start=(c == 0), stop=(c == KD - 1))),
                        r=(sqb[s], self.cstb), w=(self.psb[5],))
            oc = self.O["nconv_s"][j]
            self.dma("sp", oc[:, 0:HL - 1, :], self.I["st_conv"][j][:, 1:HL, :], key="d2d")
            self.store_tm(oc[:, HL - 1, :], uf, ufb, N, KD, stq, stqb, key="nconvs")
        (mu, mub), (msq, msqb), (var, varb), (rs, rsb), (tmp, tmpb) = scr[2], scr[3], scr[4], scr[5], scr[6]
        t1 = [scr[0][0], scr[1][0]]
        t1b = [scr[0][1], scr[1][1]]
        self.ts(mu[:, :N], ps1[:, :N], 1.0 / D, None, ALU.mult, ALU.bypass, r=(self.psb[4],), w=(mub,))
        self.tt(msq[:, :N], mu[:, :N], mu[:, :N], ALU.mult, r=(mub,), w=(msqb,))
        self.stt(var[:, :N], ps2[:, :N], 1.0 / D, msq[:, :N], ALU.mult, ALU.subtract, r=(self.psb[5], msqb), w=(varb,))
        self.act(tmp[:, :N], var[:, :N], AF.Sqrt, r=(varb, self.cstb), w=(tmpb,), bias=self.c("eps"), scale=1.0)
        self.op("dve", lambda e: e.reciprocal(rs[:, :N], tmp[:, :N]), r=(tmpb,), w=(rsb,))
        for c in range(KD):
            s = c % 2
            self.tt(t1[s][:, :N], cb[:, c, :N], mu[:, :N], ALU.subtract, r=(cbb[c], mub), w=(t1b[s],))
            self.tt(t1[s][:, :N], t1[s][:, :N], rs[:, :N], ALU.mult, r=(t1b[s], rsb), w=(t1b[s],))
            self.act(cb[:, c, :N], t1[s][:, :N], AF.Silu, r=(t1b[s], self.pvb), w=(cbb[c],),
                     bias=self.pvc(f"clb{j}", c), scale=self.pvc(f"clg{j}", c))
        for cg in range(D // CW):
            w2, w2b = self.wnext(self.wcols(W2, j, cg * CW, CW))
            for m in range(MG):
                c = cg * MG + m
                pi = m % 4
                self.proj(pi, w2, w2b, m * P, KD, cb, cbb, N)
                self.stt(x[:, c, :N], self.ps[pi][:, :N], self.pvc(f"cb2{j}", c), x[:, c, :N], ALU.add, ALU.add,
                         r=(self.psb[pi], xb[c], self.pvb), w=(xb[c],))

    def ffn(self, tcx, l):
        cfg = self.cfg
        KD, KF, N, D, FF, MG, CW, KG = cfg.KD, cfg.KF, tcx["N"], cfg.D, cfg.FF, cfg.MG, cfg.CW, cfg.KG
        x, xb = tcx["x"], tcx["xb"]
        samp = tcx["kind"] == "s"
        self.phase()
        self.wreset()
        actb_ = self.wt([P, KF, N], BF16)
        actbb = [self.wbufp("act") for _ in range(KF)]
        Hh = [self.wt([P, 2 + N], F32) for _ in range(4)]
        Hb = [self.wbufp("H") for _ in range(4)]
        tg = [self.wt([P, N], F32) for _ in range(4)]
        tgb = [self.wbufp("tg") for _ in range(4)]
        if samp:
            stf = [self.wt([P, 2, CW], F32) for _ in range(2)]
            stfb = [self.wbufp("stf") for _ in range(2)]
            hn = self.wt([P, 2 * KF, N], F32)
            hnb = [self.wbufp("hn") for _ in range(2 * KF)]
        self.rmsnorm(tcx, f"nf{l}", self.h, self.hb, ((tg[0], tgb[0]), (tg[1], tgb[1])))
        WU, WD = self.W["ffn_w_up"], self.W["ffn_w_down"]
        fw = lambda c, r_: self.pvc(f"fw{l}", c * 3 + r_)
        fbv = lambda c: self.pvc(f"fb{l}", c)
        it = 0
        CH = CW // 2
        for pg in range(FF // CH):
            wg, wgb = self.wnext(self.wcols(WU, l, pg * CH, CH), self.wcols(WU, l, FF + pg * CH, CH))
            if samp:
                ss = pg % 2
                sv = self.I["st_ffn"][l].rearrange("s r f -> (s r) f")
                self.dma("sp", stf[ss][:2 * N, 0, :CH], sv[:, pg * CH:(pg + 1) * CH], r=(), w=(stfb[ss],), key=("stf", ss))
                self.dma("sp", stf[ss][:2 * N, 1, :CH], sv[:, FF + pg * CH:FF + (pg + 1) * CH], r=(), w=(stfb[ss],),
                         key=("stf", ss))
            for m in range(CH // P):
                c = pg * (CH // P) + m
                s = it % 2
                it += 1
                for half, (wt_, wtb_, ci) in enumerate(((wg, wgb, c), (wg, wgb, KF + c))):
                    pi = 2 * half + s
                    hi = 2 * s + half
                    self.proj(pi, wt_, wtb_, half * CH + m * P, KD, self.h, self.hb, N)
                    psv = self.ps[pi]
                    if not samp:
                        Hc = Hh[hi]
                        self.cp("act", Hc[:, 0:2], self.fh[l][:, ci, :], r=(self.fhb[l][ci],), w=(Hb[hi],))
                        self.cp("act", Hc[:, 2:2 + N], psv[:, :N], r=(self.psb[pi],), w=(Hb[hi],))
                        self.cp("act", self.fh[l][:, ci, :], Hc[:, N:N + 2], r=(Hb[hi],), w=(self.fhb[l][ci],))
                        self.act(tg[hi][:, :N], psv[:, :N], AF.Identity, r=(self.psb[pi], self.pvb), w=(tgb[hi],),
                                 bias=fbv(ci), scale=fw(ci, 2))
                        self.stt(tg[hi][:, :N], Hc[:, 1:1 + N], fw(ci, 1), tg[hi][:, :N], ALU.mult, ALU.add,
                                 r=(Hb[hi], tgb[hi], self.pvb), w=(tgb[hi],))
                        self.stt(tg[hi][:, :N], Hc[:, 0:N], fw(ci, 0), tg[hi][:, :N], ALU.mult, ALU.add,
                                 r=(Hb[hi], tgb[hi], self.pvb), w=(tgb[hi],))
                    else:
                        pt = 6 + half
                        self.tr(self.ps[pt][:, :2 * N], stf[ss][:2 * N, half, m * P:(m + 1) * P],
                                self.c("ident", 2 * N)[:2 * N, :], r=(stfb[ss], self.cstb), w=(self.psb[pt],))
                        stv = self.ps[pt][:, :2 * N].rearrange("p (s r) -> p s r", r=2)
                        self.cp("act", hn[:, ci, :N], psv[:, :N], r=(self.psb[pi],), w=(hnb[ci],))
                        self.act(tg[hi][:, :N], psv[:, :N], AF.Identity, r=(self.psb[pi], self.pvb), w=(tgb[hi],),
                                 bias=fbv(ci), scale=fw(ci, 2))
                        self.stt(tg[hi][:, :N], stv[:, :, 1], fw(ci, 1), tg[hi][:, :N], ALU.mult, ALU.add,
                                 r=(self.psb[pt], tgb[hi], self.pvb), w=(tgb[hi],))
                        self.stt(tg[hi][:, :N], stv[:, :, 0], fw(ci, 0), tg[hi][:, :N], ALU.mult, ALU.add,
                                 r=(self.psb[pt], tgb[hi], self.pvb), w=(tgb[hi],))
                g_i, u_i = 2 * s, 2 * s + 1
                self.act(tg[g_i][:, :N], tg[g_i][:, :N], AF.Silu, r=(tgb[g_i],), w=(tgb[g_i],))
                self.tt(actb_[:, c, :N], tg[g_i][:, :N], tg[u_i][:, :N], ALU.mult, r=(tgb[g_i], tgb[u_i]), w=(actbb[c],))
        if not samp and tcx["last"]:
            stg = self.wt([P, 2, P], F32)
            stgb = self.wbufp("stg")
            for r_ in range(2):
                for c0 in range(0, 2 * KF, P):
                    cn = min(P, 2 * KF - c0)
                    self.tr(self.ps[6][:cn, :P], self.fh[l][:, c0:c0 + cn, r_], self.c("ident", P),
                            r=tuple(self.fhb[l][c0:c0 + cn]) + (self.cstb,), w=(self.psb[6],))
                    self.cp("dve", stg[:cn, r_, :], self.ps[6][:cn, :P], r=(self.psb[6],), w=(stgb,))
                    dv = self.O["nffn_p"][l][r_, c0 * P:(c0 + cn) * P].rearrange("(c p) -> c p", p=P)
                    self.dma("sp", dv, stg[:cn, r_, :], r=(stgb,), w=(), key="nffnp")
        if samp:
            of = self.O["nffn_s"][l]
            self.dma("sp", of[:, 0, :], self.I["st_ffn"][l][:, 1, :], key="d2d")
            stg2 = [self.wt([P, 512], F32) for _ in range(2)]
            stg2b = [self.wbufp("stg2") for _ in range(2)]
            for c0 in range(0, 2 * KF, 4):
                cn = min(4, 2 * KF - c0)
                q = (c0 // 4) % 2
                pi = 6 + q
                for k in range(cn):
                    self.tr(self.ps[pi][:N, k * P:(k + 1) * P], hn[:, c0 + k, :N], self.c("ident", P),
                            r=(hnb[c0 + k], self.cstb), w=(self.psb[pi],))
                self.cp("dve", stg2[q][:N, :cn * P], self.ps[pi][:N, :cn * P], r=(self.psb[pi],), w=(stg2b[q],))
                self.dma("sp", of[:, 1, c0 * P:(c0 + cn) * P], stg2[q][:N, :cn * P], r=(stg2b[q],), w=(),
                         key=("nffns", q))
        for mg in range(D // CW):
            for kg in range(KF // KG):
                view = WD[l, kg * KG * P:(kg + 1) * KG * P, mg * CW:(mg + 1) * CW].rearrange("(kc p) n -> p kc n", p=P)
                wd, wdb = self.wnext((view, KG, CW))
                for kc in range(KG):
                    k = kg * KG + kc
                    for jj in range(MG):
                        first = (kg == 0 and kc == 0)
                        last = (kg == KF // KG - 1 and kc == KG - 1)
                        self.op("pe", (lambda e, jj=jj, kc=kc, k=k, first=first, last=last, wd=wd: e.matmul(
                            self.ps[4 + jj][:, :N], wd[:, kc, jj * P:(jj + 1) * P], actb_[:, k, :N],
                            start=first, stop=last)), r=(wdb, actbb[k]), w=(self.psb[4 + jj],))
            for jj in range(MG):
                c = mg * MG + jj
                self.tt(x[:, c, :N], self.ps[4 + jj][:, :N], x[:, c, :N], ALU.add, r=(self.psb[4 + jj], xb[c]), w=(xb[c],))


    def hgrn_prompt(self, tcx, l):
        cfg = self.cfg
        j = l // 2
        KD, N, D, MG, CW, H = cfg.KD, tcx["N"], cfg.D, cfg.MG, cfg.CW, cfg.H
        x, xb = tcx["x"], tcx["xb"]
        self.phase()
        self.wreset()
        NCH = N // GC
        oall = self.wt([P, KD, N], BF16)
        oallb = [self.wbufp("oall") for _ in range(KD)]
        f32t = lambda nm: (self.wt([P, N], F32), self.wbufp(nm))
        bft = lambda nm: (self.wt([P, N], BF16), self.wbufp(nm))
        qf, qfb = f32t("qf")
        ff, ffb = f32t("ff")
        gg, ggb = f32t("gg")
        G, Gb = f32t("G")
        kk, kkb = f32t("kk")
        eG, eGb = gg, ggb
        enG, enGb = ff, ffb
        on, onb = qf, qfb
        rs, rsb = G, Gb
        tmp, tmpb = kk, kkb
        qt2 = [bft("qt"), bft("qt")]
        gth2 = [bft("gth"), bft("gth")]
        kt, ktb = bft("kt")
        vT, vTb = bft("vT")
        khT, khTb = bft("khT")
        osq, osqb = bft("osq")
        vtok = self.wt([P, NCH, P], BF16)
        vtokb = self.wbufp("vtok")
        khtok = self.wt([P, NCH, P], BF16)
        khtokb = self.wbufp("khtok")
        At = self.wt([P, NCH, GC], BF16)
        Atb = self.wbufp("At")
        dec = self.wt([P, 16], F32)
        decb = self.wbufp("dec")
        Ua = self.wt([P, P, NCH], F32)
        Uab = self.wbufp("Ua")
        Sa, Sab = Ua, Uab
        drep = self.wt([P, P, NCH], F32)
        drepb = self.wbufp("drep")
        Sab16 = self.wt([P, NCH, P], BF16)
        Sab16b = self.wbufp("Sab16")
        self.rmsnorm(tcx, f"nm{l}", self.h, self.hb, ((rs, rsb), (tmp, tmpb)))
        Wq, Wf, Wi, Wg, Wo = (self.W[k] for k in ("hgrn_w_q", "hgrn_w_f", "hgrn_w_i", "hgrn_w_g", "hgrn_w_o"))

        def proj_head(hd):
            wq, wqb = self.wnext(self.wcols(Wq, j, hd * P, P), self.wcols(Wf, j, hd * P, P),
                                 self.wcols(Wi, j, hd * P, P), self.wcols(Wg, j, hd * P, P))
            for i in range(4):
                self.proj(i, wq, wqb, i * P, KD, self.h, self.hb, N)

        def prep_head(hd):
            qt, qtb = qt2[hd % 2]
            gth, gthb = gth2[hd % 2]
            self.act(ff[:, :N], self.ps[1][:, :N], AF.Sigmoid, r=(self.psb[1],), w=(ffb,))
            self.cp("dve", vT[:, :N], self.ps[2][:, :N], r=(self.psb[2],), w=(vTb,))
            self.ts(ff[:, :N], ff[:, :N], self.omlv[:, j, hd:hd + 1], self.lbv[:, j, hd:hd + 1], ALU.mult, ALU.add,
                    r=(ffb, self.lbb), w=(ffb,))
            self.ts(kk[:, :N], ff[:, :N], -1.0, 1.0, ALU.mult, ALU.add, r=(ffb,), w=(kkb,))
            self.act(gg[:, :N], ff[:, :N], AF.Ln, r=(ffb,), w=(ggb,))
            self.op("dve", lambda e: e.tensor_tensor_scan(G[:, :N], self.c("rmask", N), gg[:, :N], 0.0,
                                                            ALU.mult, ALU.add), r=(ggb, self.cstb), w=(Gb,))
            self.act(enG[:, :N], G[:, :N], AF.Exp, r=(Gb,), w=(enGb,), scale=-1.0)
            self.act(dec[:, :NCH], G[:, GC - 1:N:GC], AF.Exp, r=(Gb,), w=(decb,))
            self.tt(kt[:, :N], kk[:, :N], enG[:, :N], ALU.mult, r=(kkb, enGb), w=(ktb,))
            self.tt(khT[:, :N].rearrange("p (c t) -> p c t", t=GC), kt[:, :N].rearrange("p (c t) -> p c t", t=GC),
                    dec[:, :NCH].unsqueeze(2).to_broadcast([P, NCH, GC]), ALU.mult, r=(ktb, decb), w=(khTb,))
            self.act(eG[:, :N], G[:, :N], AF.Exp, r=(Gb,), w=(eGb,))
            self.act(drep[:, :, :], G[:, GC - 1:N:GC].unsqueeze(1).to_broadcast([P, P, NCH]), AF.Exp, r=(Gb,), w=(drepb,))
            self.op("dve", lambda e: e.memset(drep[:, :, 0:1], 0.0), r=(), w=(drepb,))
            self.act(qf[:, :N], self.ps[0][:, :N], AF.Silu, r=(self.psb[0],), w=(qfb,))
            self.act(gth[:, :N], self.ps[3][:, :N], AF.Silu, r=(self.psb[3],), w=(gthb,))
            self.tt(qt[:, :N], qf[:, :N], eG[:, :N], ALU.mult, r=(qfb, eGb), w=(qtb,))

        def gla_A(hd):
            qt, qtb = qt2[hd % 2]
            Sfh, Sbh = self.Sf[j][:, hd, :], self.Sb[j][:, hd, :]
            Sfhb, Sbhb = self.Sfb[j][hd], self.Sbb[j][hd]
            p7 = self.ps[7][:, :].bitcast(BF16)
            for src, srcb_, dst, dstb_ in ((vT, vTb, vtok, vtokb), (khT, khTb, khtok, khtokb)):
                for c0 in range(0, NCH, 8):
                    cn = min(8, NCH - c0)
                    for k in range(cn):
                        ch = c0 + k
                        self.tr(p7[:GC, k * P:(k + 1) * P], src[:, ch * GC:(ch + 1) * GC], self.identb[:, :],
                                r=(srcb_, self.cstb), w=(self.psb[7],))
                    self.cp("act", dst[:GC, c0:c0 + cn, :], p7[:GC, :cn * P].rearrange("p (c k) -> p c k", k=P),
                            r=(self.psb[7],), w=(dstb_,))
            for c0 in range(0, NCH, 4):
                cn = min(4, NCH - c0)
                pi = 4 + (c0 // 4) % 2
                for k in range(cn):
                    ch = c0 + k
                    self.mm(self.ps[pi][:, k * P:(k + 1) * P], [(khtok[:GC, ch, :], vtok[:GC, ch, :])],
                            r=(khtokb, vtokb), w=(self.psb[pi],))
                self.cp("act", Ua[:, :, c0:c0 + cn].rearrange("p v c -> p c v"),
                        self.ps[pi][:, :cn * P].rearrange("p (c v) -> p c v", v=P), r=(self.psb[pi],), w=(Uab,))
            for ch in range(NCH):
                self.mm(self.ps[7][:GC, ch * GC:(ch + 1) * GC],
                        [(kt[:, ch * GC:(ch + 1) * GC], qt[:, ch * GC:(ch + 1) * GC])],
                        r=(ktb, qtb), w=(self.psb[7],))
            self.tt(At[:GC, :, :], self.ps[7][:GC, :NCH * GC].rearrange("p (c t) -> p c t", t=GC),
                    self.c("triu", GC)[:GC, :].unsqueeze(1).to_broadcast([GC, NCH, GC]), ALU.mult,
                    r=(self.psb[7], self.cstb), w=(Atb,))
            self.stt(Ua[:, :, 0], Sfh, dec[:, 0:1], Ua[:, :, 0], ALU.mult, ALU.add, r=(Sfhb, decb, Uab), w=(Uab,))
            self.op("dve", lambda e: e.tensor_tensor_scan(Sa[:, :, :].rearrange("p v c -> p (v c)"),
                                                            drep[:, :, :].rearrange("p v c -> p (v c)"),
                                                            Ua[:, :, :].rearrange("p v c -> p (v c)"), 0.0,
                                                            ALU.mult, ALU.add), r=(Uab, drepb), w=(Uab,))
            self.cp("act", Sab16[:, :, :], Sa[:, :, :].rearrange("p v c -> p c v"), r=(Sab,), w=(Sab16b,))

        def gla_B(hd):
            qt, qtb = qt2[hd % 2]
            Sfh, Sbh = self.Sf[j][:, hd, :], self.Sb[j][:, hd, :]
            Sfhb, Sbhb = self.Sfb[j][hd], self.Sbb[j][hd]
            for ch in range(NCH):
                cs_ = slice(ch * GC, (ch + 1) * GC)
                lhs = Sbh if ch == 0 else Sab16[:, ch - 1, :]
                self.mm(self.ps[6][:, cs_], [(lhs, qt[:, cs_]), (vtok[:GC, ch, :], At[:GC, ch, :])],
                        r=(Sbhb, Sab16b, qtb, vtokb, Atb), w=(self.psb[6],))
            self.cp("dve", Sfh, Sa[:, :, NCH - 1], r=(Sab,), w=(Sfhb,))
            self.cp("act", Sbh, Sab16[:, NCH - 1, :], r=(Sab16b,), w=(Sbhb,))

        def norm_head(hd):
            gth, gthb = gth2[hd % 2]
            pso, psob = self.ps[6], self.psb[6]
            self.act(osq[:, :N], pso[:, :N], AF.Square, r=(psob,), w=(osqb,))
            self.mm(self.ps[5][:, :N], [(self.onesb[:, :], osq[:, :N])], r=(osqb, self.cstb), w=(self.psb[5],))
            self.rstd_from(self.ps[5][:, :N], self.psb[5], N, 1.0 / P, rs[:, :N], rsb, tmp[:, :N], tmpb)
            self.stt(on[:, :N], pso[:, :N], self.pvc(f"hng{j}", hd), rs[:, :N], ALU.mult, ALU.mult,
                     r=(psob, rsb, self.pvb), w=(onb,))
            self.tt(oall[:, hd, :N], on[:, :N], gth[:, :N], ALU.mult, r=(onb, gthb), w=(oallb[hd],))

        proj_head(0)
        prep_head(0)
        gla_A(0)
        for hd in range(1, H):
            proj_head(hd)
            gla_B(hd - 1)
            prep_head(hd)
            norm_head(hd - 1)
            gla_A(hd)
        gla_B(H - 1)
        norm_head(H - 1)
        if tcx["last"]:
            ov = self.O["nhgrn_p"][j].rearrange("h k v -> k h v")
            self.dma("sp", ov, self.Sf[j][:, :, :], r=tuple(self.Sfb[j]), w=(), key="nhgrnp")
        for cg in range(D // CW):
            wo, wob = self.wnext(self.wcols(Wo, j, cg * CW, CW))
            for m in range(MG):
                c = cg * MG + m
                pi = m % 4
                self.proj(pi, wo, wob, m * P, KD, oall, oallb, N)
                self.tt(x[:, c, :N], self.ps[pi][:, :N], x[:, c, :N], ALU.add, r=(self.psb[pi], xb[c]), w=(xb[c],))

    def hgrn(self, tcx, l):
        cfg = self.cfg
        j = l // 2
        KD, N, D, MG, CW, H = cfg.KD, tcx["N"], cfg.D, cfg.MG, cfg.CW, cfg.H
        x, xb = tcx["x"], tcx["xb"]
        samp = tcx["kind"] == "s"
        self.phase()
        self.wreset()
        T = cfg.T
        gate = self.wt([P, KD, N], BF16)
        gateb = [self.wbufp("gate") for _ in range(KD)]
        oall = self.wt([P, KD, N], BF16)
        oallb = [self.wbufp("oall") for _ in range(KD)]
        f32t = lambda nm: (self.wt([P, N], F32), self.wbufp(nm))
        qf, qfb = f32t("qf")
        ff, ffb = f32t("ff")
        gg, ggb = f32t("gg")
        G, Gb = f32t("G")
        kk, kkb = f32t("kk")
        eG, eGb = gg, ggb
        enG, enGb = ff, ffb
        on, onb = f32t("on")
        rs, rsb = f32t("rs")
        tmp, tmpb = f32t("tmp")
        bft = lambda nm: (self.wt([P, N], BF16), self.wbufp(nm))
        qt, qtb = bft("qt")
        kt, ktb = bft("kt")
        vT, vTb = bft("vT")
        osq, osqb = bft("osq")
        if not samp:
            khT, khTb = bft("khT")
            NCH = N // GC
            vtok = self.wt([P, NCH, P], BF16)
            vtokb = self.wbufp("vtok")
            khtok = self.wt([P, NCH, P], BF16)
            khtokb = self.wbufp("khtok")
            At = self.wt([P, NCH, GC], BF16)
            Atb = self.wbufp("At")
            dec = self.wt([P, max(16, NCH)], F32)
            decb = self.wbufp("dec")
        else:
            ktok = self.wt([P, P], BF16)
            ktokb = self.wbufp("ktok")
            vtk = self.wt([P, P], BF16)
            vtkb = self.wbufp("vtk")
            vblk = self.wt([P, N, P], BF16)
            vblkb = self.wbufp("vblk")
            Sin = [self.wt([P, N, P], F32) for _ in range(2)]
            Sinb = [self.wbufp("Sin") for _ in range(2)]
            Sn = self.wt([P, N, P], F32)
            Snb = self.wbufp("Sn")
            Snbf = self.wt([P, N, P], BF16)
            Snbfb = self.wbufp("Snbf")
            qb16 = self.wt([P, 16], BF16)
            qb16b = self.wbufp("qb16")
        self.rmsnorm(tcx, f"nm{l}", self.h, self.hb, ((rs, rsb), (tmp, tmpb)))
        Wq, Wf, Wi, Wg, Wo = (self.W[k] for k in ("hgrn_w_q", "hgrn_w_f", "hgrn_w_i", "hgrn_w_g", "hgrn_w_o"))
        for hd in range(H):
            wq, wqb = self.wnext(self.wcols(Wq, j, hd * P, P), self.wcols(Wf, j, hd * P, P),
                                 self.wcols(Wi, j, hd * P, P), self.wcols(Wg, j, hd * P, P))
            for m in range(1):
                if samp:
                    s2 = hd % 2
                    sv = self.I["st_hgrn"][j][:, hd].rearrange("s k v -> k s v")
                    self.dma("sp", Sin[s2][:, :, :], sv, r=(), w=(Sinb[s2],), key=("Sin", s2))
                self.proj(0, wq, wqb, 0, KD, self.h, self.hb, N)
                self.proj(1, wq, wqb, P, KD, self.h, self.hb, N)
                self.proj(2, wq, wqb, 2 * P, KD, self.h, self.hb, N)
                self.proj(3, wq, wqb, 3 * P, KD, self.h, self.hb, N)
                self.act(qf[:, :N], self.ps[0][:, :N], AF.Silu, r=(self.psb[0],), w=(qfb,))
                self.act(gate[:, hd, :N], self.ps[3][:, :N], AF.Silu, r=(self.psb[3],), w=(gateb[hd],))
                self.act(ff[:, :N], self.ps[1][:, :N], AF.Sigmoid, r=(self.psb[1],), w=(ffb,))
                self.cp("dve", vT[:, :N], self.ps[2][:, :N], r=(self.psb[2],), w=(vTb,))
                self.ts(ff[:, :N], ff[:, :N], self.omlv[:, j, hd:hd + 1], self.lbv[:, j, hd:hd + 1], ALU.mult, ALU.add,
                        r=(ffb, self.lbb), w=(ffb,))
                self.ts(kk[:, :N], ff[:, :N], -1.0, 1.0, ALU.mult, ALU.add, r=(ffb,), w=(kkb,))
                if not samp:
                    self.act(gg[:, :N], ff[:, :N], AF.Ln, r=(ffb,), w=(ggb,))
                    self.op("dve", lambda e: e.tensor_tensor_scan(G[:, :N], self.c("rmask", N), gg[:, :N], 0.0,
                                                                    ALU.mult, ALU.add),
                            r=(ggb, self.cstb), w=(Gb,))
                    self.act(eG[:, :N], G[:, :N], AF.Exp, r=(Gb,), w=(eGb,))
                    self.act(enG[:, :N], G[:, :N], AF.Exp, r=(Gb,), w=(enGb,), scale=-1.0)
                    Glast = G[:, GC - 1:N:GC]
                    self.act(dec[:, :NCH], Glast, AF.Exp, r=(Gb,), w=(decb,))
                    self.tt(qt[:, :N], qf[:, :N], eG[:, :N], ALU.mult, r=(qfb, eGb), w=(qtb,))
                    self.tt(kt[:, :N], kk[:, :N], enG[:, :N], ALU.mult, r=(kkb, enGb), w=(ktb,))
                    self.tt(khT[:, :N].rearrange("p (c t) -> p c t", t=GC), kt[:, :N].rearrange("p (c t) -> p c t", t=GC),
                            dec[:, :NCH].unsqueeze(2).to_broadcast([P, NCH, GC]), ALU.mult, r=(ktb, decb), w=(khTb,))
                    p7 = self.ps[7][:, :].bitcast(BF16)
                    for src, srcb_, dst, dstb_ in ((vT, vTb, vtok, vtokb), (khT, khTb, khtok, khtokb)):
                        for c0 in range(0, NCH, 8):
                            cn = min(8, NCH - c0)
                            for k in range(cn):
                                ch = c0 + k
                                self.tr(p7[:GC, k * P:(k + 1) * P], src[:, ch * GC:(ch + 1) * GC], self.identb[:, :],
                                        r=(srcb_, self.cstb), w=(self.psb[7],))
                            self.cp("act", dst[:GC, c0:c0 + cn, :], p7[:GC, :cn * P].rearrange("p (c k) -> p c k", k=P),
                                    r=(self.psb[7],), w=(dstb_,))
                    for ch in range(NCH):
                        self.mm(self.ps[5][:GC, ch * GC:(ch + 1) * GC],
                                [(kt[:, ch * GC:(ch + 1) * GC], qt[:, ch * GC:(ch + 1) * GC])],
                                r=(ktb, qtb), w=(self.psb[5],))
                    self.tt(At[:GC, :, :], self.ps[5][:GC, :NCH * GC].rearrange("p (c t) -> p c t", t=GC),
                            self.c("triu", GC)[:GC, :].unsqueeze(1).to_broadcast([GC, NCH, GC]), ALU.mult,
                            r=(self.psb[5], self.cstb), w=(Atb,))
                    Sfh, Sbh = self.Sf[j][:, hd, :], self.Sb[j][:, hd, :]
                    Sfhb, Sbhb = self.Sfb[j][hd], self.Sbb[j][hd]
                    for ch in range(NCH):
                        cs_ = slice(ch * GC, (ch + 1) * GC)
                        self.mm(self.ps[6][:, cs_], [(Sbh, qt[:, cs_]), (vtok[:GC, ch, :], At[:GC, ch, :])],
                                r=(Sbhb, qtb, vtokb, Atb), w=(self.psb[6],))
                        self.mm(self.ps[4][:, :P], [(khtok[:GC, ch, :], vtok[:GC, ch, :])], r=(khtokb, vtokb),
                                w=(self.psb[4],))
                        self.stt(Sfh, Sfh, dec[:, ch:ch + 1], self.ps[4][:, :P], ALU.mult, ALU.add,
                                 r=(Sfhb, decb, self.psb[4]), w=(Sfhb,))
                        self.cp("act", Sbh, Sfh, r=(Sfhb,), w=(Sbhb,))
                    pso, psob = self.ps[6], self.psb[6]
                else:
                    self.cp("act", kt[:, :N], kk[:, :N], r=(kkb,), w=(ktb,))
                    self.cp("act", qb16[:, :N], qf[:, :N], r=(qfb,), w=(qb16b,))
                    p7 = self.ps[7][:, :].bitcast(BF16)
                    self.tr(p7[:N, 0:P], kt[:, :N], self.identb[:, :], r=(ktb, self.cstb), w=(self.psb[7],))
                    self.tr(p7[:N, P:2 * P], vT[:, :N], self.identb[:, :], r=(vTb, self.cstb), w=(self.psb[7],))
                    self.cp("act", ktok[:N, :], p7[:N, 0:P], r=(self.psb[7],), w=(ktokb,))
                    self.cp("act", vtk[:N, :], p7[:N, P:2 * P], r=(self.psb[7],), w=(vtkb,))
                    self.tt(vblk[:N, :, :], vtk[:N, :].unsqueeze(1).to_broadcast([N, N, P]),
                            self.c("ident", N)[:N, :].unsqueeze(2).to_broadcast([N, N, P]), ALU.mult,
                            r=(vtkb, self.cstb), w=(vblkb,))
                    for q4 in range(N // 4):
                        self.mm(self.ps[4 + q4 % 2][:, :],
                                [(ktok[:N, :], vblk[:N, q4 * 4:(q4 + 1) * 4, :].rearrange("p s v -> p (s v)"))],
                                r=(ktokb, vblkb), w=(self.psb[4 + q4 % 2],))
                        sl = slice(q4 * 4, (q4 + 1) * 4)
                        self.tt(Sn[:, sl, :], Sin[s2][:, sl, :], ff[:, sl].unsqueeze(2).to_broadcast([P, 4, P]), ALU.mult,
                                r=(Sinb[s2], ffb), w=(Snb,))
                        self.tt(Sn[:, sl, :], Sn[:, sl, :], self.ps[4 + q4 % 2][:, :].rearrange("p (s v) -> p s v", v=P),
                                ALU.add, r=(Snb, self.psb[4 + q4 % 2]), w=(Snb,))
                    self.cp("act", Snbf[:, :, :], Sn[:, :, :], r=(Snb,), w=(Snbfb,))
                    ov = self.O["nhgrn_s"][j][:, hd].rearrange("s k v -> k s v")
                    self.dma("sp", ov, Sn[:, :, :], r=(Snb,), w=(), key="Snout")
                    for s_ in range(N):
                        self.mm(self.ps[6][:, s_:s_ + 1], [(Snbf[:, s_, :], qb16[:, s_:s_ + 1])], r=(Snbfb, qb16b),
                                w=(self.psb[6],))
                    pso, psob = self.ps[6], self.psb[6]
                self.act(osq[:, :N], pso[:, :N], AF.Square, r=(psob,), w=(osqb,))
                self.mm(self.ps[5][:, :N], [(self.onesb[:, :], osq[:, :N])], r=(osqb, self.cstb), w=(self.psb[5],))
                self.rstd_from(self.ps[5][:, :N], self.psb[5], N, 1.0 / P, rs[:, :N], rsb, tmp[:, :N], tmpb)
                self.stt(on[:, :N], pso[:, :N], self.pvc(f"hng{j}", hd), rs[:, :N], ALU.mult, ALU.mult,
                         r=(psob, rsb, self.pvb), w=(onb,))
                self.tt(oall[:, hd, :N], on[:, :N], gate[:, hd, :N], ALU.mult, r=(onb, gateb[hd]), w=(oallb[hd],))
        if not samp and tcx["last"]:
            ov = self.O["nhgrn_p"][j].rearrange("h k v -> k h v")
            self.dma("sp", ov, self.Sf[j][:, :, :], r=tuple(self.Sfb[j]), w=(), key="nhgrnp")
        for cg in range(D // CW):
            wo, wob = self.wnext(self.wcols(Wo, j, cg * CW, CW))
            for m in range(MG):
                c = cg * MG + m
                pi = m % 4
                self.proj(pi, wo, wob, m * P, KD, oall, oallb, N)
                self.tt(x[:, c, :N], self.ps[pi][:, :N], x[:, c, :N], ALU.add, r=(self.psb[pi], xb[c]), w=(xb[c],))

    def emit(self):
        cfg = self.cfg
        KD, T, NS, D, NTL = cfg.KD, cfg.T, cfg.NS, cfg.D, cfg.NTL
        if not self.planning:
            self.pfx = ""
            self.dma("sp", self.pv[:, :], self.I["pv"], w=(self.pvb,), key="pv")
            self.dma("sp", self.cst[:, :], self.I["cst"], w=(self.cstb,), key="cst")
            self.op("dve", lambda e: e.tensor_copy(self.identb[:, :], self.c("ident", P)), r=(self.cstb,), w=(self.cstb,))
            self.op("dve", lambda e: e.memset(self.onesb[:, :], 1.0), r=(), w=(self.cstb,))
            assert cfg.NHL == 2
            self.op("dve", lambda e: e.memset(self.lbvS[:, 0, :], 0.0), r=(), w=(self.lbb,))
            self.tt(self.lbvS[:, 1, :], self.pvc("hraw1", 0, KD), self.pvc("hraw0", 0, KD), ALU.subtract,
                    r=(self.pvb,), w=(self.lbb,))
            self.act(self.lbvS[:, 1, :], self.lbvS[:, 1, :], AF.Sigmoid, r=(self.lbb,), w=(self.lbb,))
            self.ts(self.lbvP[:, 0, :], self.lbvS[:, 1, :], self.pvc("mB", 0, 1), None, ALU.mult, ALU.bypass,
                    r=(self.lbb, self.pvb), w=(self.lbb,))
            self.ts(self.omlvS[:, :, :], self.lbvS[:, :, :], -1.0, 1.0, ALU.mult, ALU.add, r=(self.lbb,), w=(self.lbb,))
            self.ts(self.omlvP[:, :, :], self.lbvP[:, :, :], -1.0, 1.0, ALU.mult, ALU.add, r=(self.lbb,), w=(self.lbb,))
            self.op("dve", lambda e: e.memset(self.Sf[0][:, :, :], 0.0), w=tuple(self.Sfb[0]))
            self.op("dve", lambda e: e.memset(self.Sb[0][:, :, :], 0.0), w=tuple(self.Sbb[0]))
            self.op("dve", lambda e: e.memset(self.uh[0][:, :, :], 0.0), w=(self.uhb[0],))
            for l in range(2):
                self.op("dve", lambda e, l=l: e.memset(self.fh[l][:, :, :], 0.0), w=tuple(self.fhb[l]))
        NSTEP = NTL + 1
        for s in range(NSTEP):
            ti = min(s, NTL - 1)
            oslot = 1 if s == NTL else 0
            tcx = dict(kind="p", N=T, ti=ti, last=(s >= NTL - 1), x=self.x, xb=self.xb)
            self.set_mode("P", oslot)
            self.phase()
            self.wreset()
            stg = [self.wt([P, D], F32) for _ in range(2)]
            stgb = [self.wbufp("xin") for _ in range(2)]
            for b in range(T // P):
                rows = self.I["xp"][ti * T + b * P:ti * T + (b + 1) * P, :]
                self.load_fm(self.x, self.xb, rows, P, KD, stg[b % 2], stgb[b % 2], key=("xin", b % 2), dst_col0=b * P)
            if s >= 1:
                xin2 = self.wt([P, KD, T], F32)
                xin2b = self.wbufp("xin2")
                KQ = KD // self.NQ
                for q in range(self.NQ):
                    g = self.xg[(s - 1) % 2][q]
                    self.dma("sp", xin2[:, q * KQ:(q + 1) * KQ, :], g[0:P, :].rearrange("p (k t) -> p k t", k=KQ),
                             r=(self.xgb[(s - 1) % 2][q],), w=(xin2b,), key="xin2")
                for kc in range(KD):
                    self.ts(self.x[:, kc, :], self.x[:, kc, :], self.pvc("mA", 0, 1), None, ALU.mult, ALU.bypass,
                            r=(self.xb[kc], self.pvb), w=(self.xb[kc],))
                    self.stt(self.x[:, kc, :], xin2[:, kc, :], self.pvc("mB", 0, 1), self.x[:, kc, :], ALU.mult, ALU.add,
                             r=(xin2b, self.xb[kc], self.pvb), w=(self.xb[kc],))
            self.conv_mixer(tcx, 0)
            self.ffn(tcx, 0)
            if getattr(cfg, "OLDHGRN", False):
                self.hgrn(tcx, 1)
            else:
                self.hgrn_prompt(tcx, 1)
            self.ffn(tcx, 1)
            if s == 0:
                mA = self.pvc("mA", 0, 1)
                self.ts(self.uh[0][:, :, :], self.uh[0][:, :, :], mA, None, ALU.mult, ALU.bypass,
                        r=(self.uhb[0], self.pvb), w=(self.uhb[0],))
                self.ts(self.Sf[0][:, :, :], self.Sf[0][:, :, :], mA, None, ALU.mult, ALU.bypass,
                        r=tuple(self.Sfb[0]) + (self.pvb,), w=tuple(self.Sfb[0]))
                self.ts(self.Sb[0][:, :, :], self.Sb[0][:, :, :], mA, None, ALU.mult, ALU.bypass,
                        r=tuple(self.Sbb[0]) + (self.pvb,), w=tuple(self.Sbb[0]))
                for l in range(2):
                    self.ts(self.fh[l][:, :, :], self.fh[l][:, :, :], mA, None, ALU.mult, ALU.bypass,
                            r=tuple(self.fhb[l]) + (self.pvb,), w=tuple(self.fhb[l]))
            if s < NSTEP - 1:
                KQ = KD // self.NQ
                for q in range(self.NQ):
                    self.dma("sp", self.xo[q][:, :].rearrange("p (k t) -> p k t", k=KQ), self.x[:, q * KQ:(q + 1) * KQ, :],
                             r=tuple(self.xb[q * KQ:(q + 1) * KQ]), w=(self.xob[q],), key=("xo", q))
                    self.coll_allgather(self.xo[q].opt(), self.xg[s % 2][q].opt(), r=(self.xob[q],),
                                        w=(self.xgb[s % 2][q],))
            if s >= 1:
                self.phase()
                self.wreset()
                yn = self.wt([P, KD, T], F32)
                ynb = [self.wbufp("yn") for _ in range(KD)]
                stg = [self.wt([P, 512], F32) for _ in range(2)]
                stgb = [self.wbufp("yout") for _ in range(2)]
                scr = ((self.wt([P, T], F32), self.wbufp("scr")), (self.wt([P, T], F32), self.wbufp("scr")))
                self.rmsnorm(tcx, "nfin", yn, ynb, scr)
                for b in range(T // P):
                    rows = self.O["y"][(s - 1) * T + b * P:(s - 1) * T + (b + 1) * P, :]
                    self.store_tm(rows, yn, ynb, P, KD, stg, stgb, key="yout", src_col0=b * P)
        self.set_mode("S")
        tcx = dict(kind="s", N=NS, ti=0, last=True, x=self.x, xb=self.xb)
        self.phase()
        self.wreset()
        stg = [self.wt([P, D], F32) for _ in range(2)]
        stgb = [self.wbufp("xin") for _ in range(2)]
        self.load_fm(self.x, self.xb, self.I["xs"], NS, KD, stg[0], stgb[0], key=("xin", 0))
        for l in range(cfg.DEPTH):
            if l % 2 == 0:
                self.conv_mixer(tcx, l)
            else:
                self.hgrn(tcx, l)
            self.ffn(tcx, l)
        self.phase()
        self.wreset()
        yn = self.wt([P, KD, NS], F32)
        ynb = [self.wbufp("yn") for _ in range(KD)]
        stg = [self.wt([P, 512], F32) for _ in range(2)]
        stgb = [self.wbufp("yout") for _ in range(2)]
        scr = ((self.wt([P, NS], F32), self.wbufp("scr")), (self.wt([P, NS], F32), self.wbufp("scr")))
        self.rmsnorm(tcx, "nfin", yn, ynb, scr)
        self.store_tm(self.O["ys"], yn, ynb, NS, KD, stg, stgb, key="yout")
        if not self.planning:
            sp = self.eng["sp"]
            for key, (k, v) in self.dsems.items():
                if sp.seen.get(k, 0) < v:
                    sp.obj.wait_ge(self.sems[k], v)
                    sp.seen[k] = v
            for en in ("pe", "act", "dve", "pool"):
                e = self.eng[en]
                if e.sem is not None and e.count > 0:
                    sp.obj.wait_ge(self.sems[e.sem], e.count)

    def build(self):
        cfg, nc = self.cfg, self.nc
        D, FF, NS, KD, T = cfg.D, cfg.FF, cfg.NS, cfg.KD, cfg.T
        di = lambda n, s: nc.dram_tensor(n, list(s), F32, kind="ExternalInput").ap()
        do = lambda n, s: nc.dram_tensor(n, list(s), F32, kind="ExternalOutput").ap()
        self.I = {
            "xp": di("xp", [cfg.SEQ, D]), "xs": di("xs", [NS, D]),
            "st_conv": di("st_conv", [cfg.NCL, NS, CWID - 1, D]),
            "st_hgrn": di("st_hgrn", [cfg.NHL, NS, cfg.H, P, P]),
            "st_ffn": di("st_ffn", [cfg.DEPTH, NS, 2, 2 * FF]),
            "pv": di("pv", [P, self.npv]), "cst": di("cst", [P, self.ncst]),
        }
        self.WS = {
            "conv_w_pw1": di("conv_w_pw1", [cfg.NCL, D, 2 * D]), "conv_w_pw2": di("conv_w_pw2", [cfg.NCL, D, D]),
            "hgrn_w_q": di("hgrn_w_q", [cfg.NHL, D, D]), "hgrn_w_f": di("hgrn_w_f", [cfg.NHL, D, D]),
            "hgrn_w_i": di("hgrn_w_i", [cfg.NHL, D, D]), "hgrn_w_g": di("hgrn_w_g", [cfg.NHL, D, D]),
            "hgrn_w_o": di("hgrn_w_o", [cfg.NHL, D, D]),
            "ffn_w_up": di("ffn_w_up", [cfg.DEPTH, D, 2 * FF]), "ffn_w_down": di("ffn_w_down", [cfg.DEPTH, FF, D]),
        }
        self.WP = {
            "conv_w_pw1": di("p_conv_w_pw1", [1, D, 2 * D]), "conv_w_pw2": di("p_conv_w_pw2", [1, D, D]),
            "hgrn_w_q": di("p_hgrn_w_q", [1, D, D]), "hgrn_w_f": di("p_hgrn_w_f", [1, D, D]),
            "hgrn_w_i": di("p_hgrn_w_i", [1, D, D]), "hgrn_w_g": di("p_hgrn_w_g", [1, D, D]),
            "hgrn_w_o": di("p_hgrn_w_o", [1, D, D]),
            "ffn_w_up": di("p_ffn_w_up", [2, D, 2 * FF]), "ffn_w_down": di("p_ffn_w_down", [2, FF, D]),
        }
        self.W = self.WS
        self.Oall = {
            "nconv_p": do("nconv_p", [2, CWID - 1, D]), "nhgrn_p": do("nhgrn_p", [2, cfg.H, P, P]),
            "nffn_p": do("nffn_p", [2, 2, 2, 2 * FF]),
        }
        self.O = {
            "y": do("y", [cfg.SEQ, D]), "ys": do("ys", [NS, D]),
            "nconv_s": do("nconv_s", [cfg.NCL, NS, CWID - 1, D]), "nhgrn_s": do("nhgrn_s", [cfg.NHL, NS, cfg.H, P, P]),
            "nffn_s": do("nffn_s", [cfg.DEPTH, NS, 2, 2 * FF]),
        }
        NQ = min(4, KD)
        KQ = KD // NQ
        self.xo = [nc.dram_tensor(f"xo{q}", [P, KQ * T], F32).ap() for q in range(NQ)]
        self.xg = [[nc.dram_tensor(f"xg{i}_{q}", [2 * P, KQ * T], F32).ap() for q in range(NQ)] for i in range(2)]
        self.alloc()
        self.planning = True
        self.plan = []
        self.emit()
        self.planning = False
        self.wi = 0
        self.wissued = 0
        self.woff = 0
        self.emit()
        assert self.wi == len(self.plan), (self.wi, len(self.plan))
        return nc


def run(cfg, inp, n_cores=8, trace=False):
    D, FF = cfg.D, cfg.FF
    pvs = [pack_params(cfg, inp, c) for c in range(n_cores)]
    pv0 = pvs[0].build()
    coff, cst = make_consts(cfg)
    b = Builder(cfg, pvs[0].off, pv0.shape[1], coff, cst.shape[1])
    nc = b.build()
    B = inp["x_prompt"].shape[0]
    assert n_cores == 2 * B
    NS = cfg.NS
    wnames = ["conv_w_pw1", "conv_w_pw2", "hgrn_w_q", "hgrn_w_f", "hgrn_w_i", "hgrn_w_g", "hgrn_w_o", "ffn_w_up", "ffn_w_down"]
    in_maps = []
    for c in range(n_cores):
        isB = c % 2
        m = {
            "xp": np.ascontiguousarray(inp["x_prompt"][c // 2]),
            "xs": np.ascontiguousarray(inp["x_sample"][c * NS:(c + 1) * NS, 0, :]),
            "st_conv": np.ascontiguousarray(inp["state_conv"][:, c * NS:(c + 1) * NS]),
            "st_hgrn": np.ascontiguousarray(inp["state_hgrn"][:, c * NS:(c + 1) * NS]),
            "st_ffn": np.ascontiguousarray(inp["state_ffn"][:, c * NS:(c + 1) * NS]),
            "pv": pvs[c].build(), "cst": cst,
        }
        for w in wnames:
            m[w] = inp[w]
            if w.startswith("ffn"):
                m["p_" + w] = np.ascontiguousarray(inp[w][2 * isB:2 * isB + 2])
            else:
                m["p_" + w] = np.ascontiguousarray(inp[w][isB:isB + 1])
        in_maps.append(m)
    res = run_bass_kernel_spmd(nc, in_maps, core_ids=list(range(n_cores)), trace=trace)
    R = res.results
    y_prompt = np.stack([R[2 * b + 1]["y"] for b in range(B)], axis=0)
    y_sample = np.concatenate([R[c]["ys"] for c in range(n_cores)], axis=0)[:, None, :]
    conv_p = np.stack([np.stack([R[2 * b + j]["nconv_p"][j] for b in range(B)], axis=0) for j in range(2)], axis=0)
    hgrn_p = np.stack([np.stack([R[2 * b + j]["nhgrn_p"][j] for b in range(B)], axis=0) for j in range(2)], axis=0)
    ffn_p = np.stack([np.stack([R[2 * b + l // 2]["nffn_p"][l // 2][l % 2] for b in range(B)], axis=0)
                      for l in range(4)], axis=0)
    conv_s = np.concatenate([R[c]["nconv_s"] for c in range(n_cores)], axis=1)
    hgrn_s = np.concatenate([R[c]["nhgrn_s"] for c in range(n_cores)], axis=1)
    ffn_s = np.concatenate([R[c]["nffn_s"] for c in range(n_cores)], axis=1)
    out = (y_prompt, y_sample, conv_p, hgrn_p, ffn_p, conv_s, hgrn_s, ffn_s)
    return tuple(np.ascontiguousarray(o, dtype=np.float32) for o in out), res


def kernel(**inputs):
    inp = {k: np.asarray(v) for k, v in inputs.items()}
    cfg = Cfg()
    out, _ = run(cfg, inp)
    return out
```

```python
import numpy as np
import concourse.bass as bass
import concourse.mybir as mybir
from concourse.bass_utils import run_bass_kernel_spmd

F32 = mybir.dt.float32
BF16 = mybir.dt.bfloat16
AF = mybir.ActivationFunctionType
ALU = mybir.AluOpType
AX = mybir.AxisListType
P = 128
CWID = 31
EPS = 1e-6
GC = 32


class Cfg:
    def __init__(self, D=2048, FF=5632, SEQ=2048, T=512, NS=16, DEPTH=4):
        self.D, self.FF, self.SEQ, self.T, self.NS, self.DEPTH = D, FF, SEQ, T, NS, DEPTH
        self.KD = D // P
        self.KF = FF // P
        self.H = self.KD
        self.NTL = SEQ // T
        self.NCL = (DEPTH + 1) // 2
        self.NHL = DEPTH // 2
        self.CW = min(512, D)
        self.MG = self.CW // P
        kg = 1
        for d in range(1, 17):
            if self.KF % d == 0:
                kg = d
        self.KG = kg
        self.WK = max(self.KD, self.KG)


class PV:
    def __init__(self):
        self.cols = []
        self.off = {}
        self.n = 0

    def add(self, name, arr2d):
        self.off[name] = self.n
        self.cols.append(np.ascontiguousarray(arr2d, dtype=np.float32))
        self.n += arr2d.shape[1]

    def vec(self, name, v):
        self.add(name, np.asarray(v).reshape(-1, P).T)

    def taps(self, name, w):
        nt = w.shape[0]
        a = np.asarray(w).reshape(nt, -1, P).transpose(2, 1, 0)
        self.add(name, a.reshape(P, -1))

    def build(self):
        return np.ascontiguousarray(np.concatenate(self.cols, axis=1))


def pack_params(cfg, inp, core=0):
    pv = PV()
    D, FF = cfg.D, cfg.FF
    isB = core % 2
    gl = [2 * isB, 2 * isB + 1]
    gj = isB
    for ll, l in enumerate(gl):
        pv.vec(f"Pnm{ll}", inp["norm_mix"][l])
        pv.vec(f"Pnf{ll}", inp["norm_ffn"][l])
        pv.taps(f"Pfw{ll}", inp["ffn_w_dw"][l])
        pv.vec(f"Pfb{ll}", inp["ffn_b_dw"][l])
    pv.vec("Pcb1a0", inp["conv_b_pw1"][gj][:D])
    pv.vec("Pcb1g0", inp["conv_b_pw1"][gj][D:])
    pv.taps("Pcw0", inp["conv_w_dw"][gj])
    pv.vec("Pcbd0", inp["conv_b_dw"][gj])
    pv.vec("Pclg0", inp["conv_ln_g"][gj])
    pv.vec("Pclb0", inp["conv_ln_b"][gj])
    pv.vec("Pcb20", inp["conv_b_pw2"][gj])
    pv.vec("Phng0", inp["hgrn_norm_g"][gj])
    pv.vec("Pnfin", inp["norm_final"])
    for nm in ("mA", "PmA"):
        pv.add(nm, np.full((P, 1), 1.0 - isB))
    for nm in ("mB", "PmB"):
        pv.add(nm, np.full((P, 1), float(isB)))
    for l in range(cfg.DEPTH):
        pv.vec(f"nm{l}", inp["norm_mix"][l])
        pv.vec(f"nf{l}", inp["norm_ffn"][l])
        pv.taps(f"fw{l}", inp["ffn_w_dw"][l])
        pv.vec(f"fb{l}", inp["ffn_b_dw"][l])
    for j in range(cfg.NCL):
        pv.vec(f"cb1a{j}", inp["conv_b_pw1"][j][:D])
        pv.vec(f"cb1g{j}", inp["conv_b_pw1"][j][D:])
        pv.taps(f"cw{j}", inp["conv_w_dw"][j])
        pv.vec(f"cbd{j}", inp["conv_b_dw"][j])
        pv.vec(f"clg{j}", inp["conv_ln_g"][j])
        pv.vec(f"clb{j}", inp["conv_ln_b"][j])
        pv.vec(f"cb2{j}", inp["conv_b_pw2"][j])
    for j in range(cfg.NHL):
        pv.vec(f"hraw{j}", inp["hgrn_lb_raw"][j])
        pv.vec(f"hng{j}", inp["hgrn_norm_g"][j])
    pv.vec("nfin", inp["norm_final"])
    return pv


def make_consts(cfg):
    c = {}
    cols = []
    n = 0

    def add(name, a):
        nonlocal n
        c[name] = n
        cols.append(a.astype(np.float32))
        n += a.shape[1]

    add("ident", np.eye(P))
    tri = np.zeros((P, GC))
    tri[:GC] = np.triu(np.ones((GC, GC)))
    add("triu", tri)
    rm = np.ones((P, 512))
    rm[:, ::GC] = 0.0
    add("rmask", rm)
    add("ones", np.ones((P, 64)))
    add("eps", np.full((P, 1), EPS))
    add("one", np.ones((P, 1)))
    return c, np.ascontiguousarray(np.concatenate(cols, axis=1)).astype(np.float32)


class Buf:
    __slots__ = ("name", "w", "r")

    def __init__(self, name):
        self.name = name
        self.w = None
        self.r = []


class Eng:
    def __init__(self, name, obj):
        self.name, self.obj = name, obj
        self.sem = None
        self.count = 0
        self.seen = {}


class Builder:
    def __init__(self, cfg, pvoff, npv, coff, ncst):
        self.cfg = cfg
        self.pvoff, self.npv, self.coff, self.ncst = pvoff, npv, coff, ncst
        self.nc = nc = bass.Bass("TRN2", target_bir_lowering=False)
        self.eng = {
            "pe": Eng("pe", nc.tensor), "act": Eng("act", nc.scalar), "dve": Eng("dve", nc.vector),
            "pool": Eng("pool", nc.gpsimd), "sp": Eng("sp", nc.sync),
        }
        self.sems = []
        self.dsems = {}
        self.planning = False
        self.pfx = ""
        self.plan = []
        self.nbuf = 0
        self.phase_bufs = []
        self.prev_phase_tokens = []

    def newsem(self, name):
        h = self.nc.alloc_semaphore(name)
        self.sems.append(h)
        return len(self.sems) - 1

    def buf(self, name="b"):
        self.nbuf += 1
        return Buf(f"{name}{self.nbuf}")

    def bufs(self, n, name="b"):
        return [self.buf(name) for _ in range(n)]

    def wbuf(self, name="w"):
        b = self.buf(name)
        self.phase_bufs.append(b)
        return b

    def _waits(self, e, r, w):
        need = {}

        def add(tok):
            if tok is None:
                return
            k, v = tok
            if need.get(k, 0) < v:
                need[k] = v

        for b in r:
            add(b.w)
        for b in w:
            add(b.w)
            for t in b.r:
                add(t)
        for k, v in need.items():
            if e.name == "pe" and k == e.sem:
                continue
            if e.seen.get(k, 0) < v:
                e.obj.wait_ge(self.sems[k], v)
                e.seen[k] = v

    def _commit(self, tok, r, w):
        for b in r:
            b.r.append(tok)
            if len(b.r) > 24:
                m = {}
                for k, v in b.r:
                    if m.get(k, 0) < v:
                        m[k] = v
                b.r = list(m.items())
        for b in w:
            b.w = tok
            b.r = []

    def op(self, E, fn, r=(), w=()):
        if self.planning:
            return None
        e = self.eng[E]
        if e.sem is None or e.count >= 30000:
            e.sem = self.newsem(f"s_{e.name}_{len(self.sems)}")
            e.count = 0
        self._waits(e, r, w)
        ins = fn(e.obj)
        e.count += 1
        ins.then_inc(self.sems[e.sem], 1)
        tok = (e.sem, e.count)
        self._commit(tok, r, w)
        return tok

    def dma(self, E, out, in_, r=(), w=(), key=None):
        if self.planning:
            return None
        e = self.eng[E]
        self._waits(e, r, w)
        if key not in self.dsems:
            self.dsems[key] = [self.newsem(f"d_{len(self.sems)}"), 0]
        d = self.dsems[key]
        ins = e.obj.dma_start(out=out, in_=in_)
        d[1] += 16
        ins.then_inc(self.sems[d[0]], 16)
        tok = (d[0], d[1])
        self._commit(tok, r, w)
        return tok

    def phase(self):
        toks = []
        for b in self.phase_bufs:
            if b.w is not None:
                toks.append(b.w)
            toks.extend(b.r)
        m = {}
        for k, v in toks:
            if m.get(k, 0) < v:
                m[k] = v
        self.prev_phase_tokens = list(m.items())
        self.phase_bufs = []

    def wbufp(self, name="w"):
        b = self.wbuf(name)
        b.r = list(self.prev_phase_tokens)
        return b

    def act(self, out, in_, func, r, w, bias=None, scale=None):
        kw = {}
        if bias is not None:
            kw["bias"] = bias
        if scale is not None:
            kw["scale"] = scale
        return self.op("act", lambda e: e.activation(out, in_, func, **kw), r, w)

    def ts(self, out, in0, s1, s2, op0, op1, r, w, E="dve"):
        return self.op(E, lambda e: e.tensor_scalar(out, in0, s1, s2, op0, op1), r, w)

    def stt(self, out, in0, sc, in1, op0, op1, r, w):
        return self.op("dve", lambda e: e.scalar_tensor_tensor(out, in0, sc, in1, op0, op1), r, w)

    def tt(self, out, in0, in1, op, r, w, E="dve"):
        return self.op(E, lambda e: e.tensor_tensor(out, in0, in1, op), r, w)

    def cp(self, E, out, in_, r, w):
        if E == "act":
            return self.op("act", lambda e: e.copy(out, in_), r, w)
        return self.op(E, lambda e: e.tensor_copy(out, in_), r, w)

    def mm(self, out, pairs, r, w, transpose=False):
        n = len(pairs)

        def fn(e):
            ins = None
            for i, (a, b) in enumerate(pairs):
                ins = e.matmul(out, a, b, start=(i == 0), stop=(i == n - 1))
            return ins

        return self.op("pe", fn, r, w)

    def tr(self, out, in_, ident, r, w):
        return self.op("pe", lambda e: e.transpose(out, in_, ident), r, w)

    def pvc(self, name, k=0, n=1):
        o = self.pvoff[self.pfx + name] + k
        return self.pv[:, o:o + n]

    def coll_allgather(self, in_ap, out_ap, r, w):
        if self.planning:
            return None
        e = self.eng["pool"]
        self._waits(e, r, w)
        if "cc" not in self.dsems:
            self.dsems["cc"] = [self.newsem("cc"), 0]
        d = self.dsems["cc"]
        ins = e.obj.collective_compute("AllGather", ALU.bypass, replica_groups=[[0, 1], [2, 3], [4, 5], [6, 7]],
                                       ins=[in_ap], outs=[out_ap])
        d[1] += 1
        ins.then_inc(self.sems[d[0]], 1)
        tok = (d[0], d[1])
        self._commit(tok, r, w)
        return tok

    def set_mode(self, mode, oslot=0):
        if mode == "P":
            self.W, self.pfx = self.WP, "P"
            self.lbv, self.omlv = self.lbvP, self.omlvP
            self.O["nconv_p"] = self.Oall["nconv_p"][oslot:oslot + 1]
            self.O["nhgrn_p"] = self.Oall["nhgrn_p"][oslot:oslot + 1]
            self.O["nffn_p"] = self.Oall["nffn_p"][oslot]
        else:
            self.W, self.pfx = self.WS, ""
            self.lbv, self.omlv = self.lbvS, self.omlvS

    def wnext(self, *pieces):
        if self.planning:
            self.plan.append(pieces)
            return self.wslot[0], self.wslotb[0]
        i = self.wi
        NSL = len(self.wslot)
        while self.wissued < min(i + NSL, len(self.plan)):
            j = self.wissued
            s = j % NSL
            c0 = 0
            for (v, k2, c2) in self.plan[j]:
                self.dma("pool", self.wslot[s][:, :k2, c0:c0 + c2], v, r=(), w=(self.wslotb[s],), key=("w", s))
                c0 += c2
            self.wissued += 1
        self.wi += 1
        return self.wslot[i % NSL], self.wslotb[i % NSL]

    def wcols(self, W, l, c0, cols):
        K = W.shape[1]
        return W[l, :, c0:c0 + cols].rearrange("(kc p) n -> p kc n", p=P), K // P, cols

    def alloc(self):
        cfg, nc = self.cfg, self.nc
        KD, KF, T, NS, D, FF = cfg.KD, cfg.KF, cfg.T, cfg.NS, cfg.D, cfg.FF
        A = nc.alloc_sbuf_tensor
        self.pv = A("pv_sb", [P, self.npv], F32)
        self.pvb = self.buf("pv")
        self.cst = A("cst_sb", [P, self.ncst], F32)
        self.cstb = self.buf("cst")
        self.identb = A("identb", [P, P], BF16)
        self.onesb = A("onesb", [P, P], BF16)
        self.lbvS = A("lbvS", [P, cfg.NHL, KD], F32)
        self.omlvS = A("omlvS", [P, cfg.NHL, KD], F32)
        self.lbvP = A("lbvP", [P, 1, KD], F32)
        self.omlvP = A("omlvP", [P, 1, KD], F32)
        self.lbv, self.omlv = self.lbvS, self.omlvS
        self.lbb = self.buf("lb")
        self.x = A("x_sb", [P, KD, T], F32)
        self.xb = self.bufs(KD, "x")
        self.h = A("h_sb", [P, KD, T], BF16)
        self.hb = self.bufs(KD, "h")
        self.wslot = [A(f"ws{i}", [P, cfg.WK, 512], BF16) for i in range(3)]
        self.wslotb = self.bufs(3, "ws")
        self.uh = [A(f"uh{j}", [P, KD, CWID - 1], BF16) for j in range(1)]
        self.uhb = self.bufs(1, "uh")
        self.Sf = [A(f"Sf{j}", [P, KD, P], F32) for j in range(1)]
        self.Sb = [A(f"Sb{j}", [P, KD, P], BF16) for j in range(1)]
        self.Sfb = [self.bufs(KD, "Sf") for j in range(1)]
        self.Sbb = [self.bufs(KD, "Sb") for j in range(1)]
        self.fh = [A(f"fh{l}", [P, 2 * KF, 2], F32) for l in range(2)]
        self.fhb = [self.bufs(2 * KF, "fh") for l in range(2)]
        self.NQ = min(4, KD)
        self.xob = self.bufs(self.NQ, "xo")
        self.xgb = [self.bufs(self.NQ, "xg") for _ in range(2)]
        self.WB = 66 * 1024
        self.work = A("work", [P, self.WB // 2], BF16)
        self.woff = 0
        self.ps = [nc.alloc_psum_tensor(f"ps{i}", [P, 512], F32) for i in range(8)]
        self.psb = self.bufs(8, "ps")

    def wreset(self):
        self.woff = 0

    def wt(self, shape, dt):
        n = int(np.prod(shape[1:]))
        nb = n * (4 if dt == F32 else 2)
        nb = (nb + 31) // 32 * 32
        assert self.woff + nb <= self.WB, (self.woff, nb, self.WB)
        a = self.work[:, self.woff // 2:(self.woff + nb) // 2]
        self.woff += nb
        if dt == F32:
            a = a.bitcast(F32)
        a = a[:, :n]
        if len(shape) == 3:
            a = a.rearrange("p (a b) -> p a b", a=shape[1])
        elif len(shape) == 4:
            a = a.rearrange("p (a b c) -> p a b c", a=shape[1], b=shape[2])
        return a

    def c(self, name, n=1, k=0):
        o = self.coff[name] + k
        return self.cst[:, o:o + n]

    def rstd_from(self, ps_ap, psbuf, n, scale, out, outb, tmp, tmpb):
        self.act(tmp, ps_ap, AF.Sqrt, r=(psbuf, self.cstb), w=(tmpb,), bias=self.c("eps"), scale=scale)
        self.op("dve", lambda e: e.reciprocal(out, tmp), r=(tmpb,), w=(outb,))

    def rmsnorm(self, tcx, wname, dst, dstb, scr):
        cfg = self.cfg
        KD, N = cfg.KD, tcx["N"]
        x, xb = tcx["x"], tcx["xb"]
        sq = [self.wt([P, N], BF16) for _ in range(2)]
        sqb = [self.wbufp("sq") for _ in range(2)]
        (rs, rsb), (tmp, tmpb) = scr
        pss, pssb = self.ps[4], self.psb[4]
        toks = []
        for kc in range(KD):
            s = kc % 2
            self.act(sq[s][:, :N], x[:, kc, :N], AF.Square, r=(xb[kc],), w=(sqb[s],))
            self.op("pe", (lambda e, s=s, kc=kc: e.matmul(pss[:, :N], self.onesb[:, :], sq[s][:, :N],
                                                       start=(kc == 0), stop=(kc == KD - 1))),
                    r=(sqb[s], self.cstb), w=(pssb,))
        self.rstd_from(pss[:, :N], pssb, N, 1.0 / cfg.D, rs[:, :N], rsb, tmp[:, :N], tmpb)
        for kc in range(KD):
            self.stt(dst[:, kc, :N], x[:, kc, :N], self.pvc(wname, kc), rs[:, :N], ALU.mult, ALU.mult,
                     r=(xb[kc], rsb, self.pvb), w=(dstb[kc],))

    def proj(self, ps_i, wt_, wtb, col0, KC, rhs, rhsb, N):
        pairs = [(wt_[:, kc, col0:col0 + P], rhs[:, kc, :N]) for kc in range(KC)]
        return self.mm(self.ps[ps_i][:, :N], pairs, r=(wtb,) + tuple(rhsb[:KC]), w=(self.psb[ps_i],))

    def load_fm(self, dst, dstb, src_rows, n, nchunks, stage, stageb, key, dst_col0=0):
        self.dma("sp", stage[:n, :nchunks * P], src_rows, r=(), w=(stageb,), key=key)
        for k0 in range(0, nchunks, 4):
            kn = min(4, nchunks - k0)
            pi = 6 + (k0 // 4) % 2
            for k in range(kn):
                self.tr(self.ps[pi][:, k * P:k * P + n], stage[:n, (k0 + k) * P:(k0 + k + 1) * P],
                        self.c("ident", n)[:n, :], r=(stageb, self.cstb), w=(self.psb[pi],))
            src = self.ps[pi][:, :kn * P].rearrange("p (k t) -> p k t", k=kn)[:, :, :n]
            self.cp("act" if (k0 // 4) % 2 == 0 else "dve", dst[:, k0:k0 + kn, dst_col0:dst_col0 + n], src,
                    r=(self.psb[pi],), w=tuple(dstb[k0:k0 + kn]))

    def store_tm(self, dst_rows, src, srcb, n, nchunks, stage, stageb, key, src_col0=0):
        for k0 in range(0, nchunks, 4):
            kn = min(4, nchunks - k0)
            q = (k0 // 4) % 2
            pi = 6 + q
            for k in range(kn):
                self.tr(self.ps[pi][:n, k * P:(k + 1) * P], src[:, k0 + k, src_col0:src_col0 + n],
                        self.c("ident", P), r=(srcb[k0 + k], self.cstb), w=(self.psb[pi],))
            self.cp("act" if q == 0 else "dve", stage[q][:n, :kn * P],
                    self.ps[pi][:n, :kn * P], r=(self.psb[pi],), w=(stageb[q],))
            self.dma("sp", dst_rows[:, k0 * P:(k0 + kn) * P], stage[q][:n, :kn * P], r=(stageb[q],), w=(),
                     key=(key, q))

    def conv_mixer(self, tcx, l):
        cfg = self.cfg
        j = l // 2
        KD, N, D, MG, CW = cfg.KD, tcx["N"], cfg.D, cfg.MG, cfg.CW
        x, xb = tcx["x"], tcx["xb"]
        samp = tcx["kind"] == "s"
        self.phase()
        self.wreset()
        HL = CWID - 1
        ub = self.wt([P, KD, HL + N], BF16)
        ubb = [self.wbufp("ub") for _ in range(KD)]
        cb, cbb = self.h, self.hb
        uf = self.wt([P, KD, HL if not samp else N], F32)
        ufb = [self.wbufp("uf") for _ in range(KD)]
        scr = [(self.wt([P, N], F32), self.wbufp("scr")) for _ in range(8)]
        sg = [scr[0][0], scr[1][0]]
        sgb = [scr[0][1], scr[1][1]]
        stq = [self.wt([P, 512], F32) for _ in range(2)]
        stqb = [self.wbufp("stq") for _ in range(2)]
        self.rmsnorm(tcx, f"nm{l}", self.h, self.hb, (scr[2], scr[3]))
        if not samp:
            for kc in range(KD):
                self.cp("act", ub[:, kc, 0:HL], self.uh[j][:, kc, :], r=(self.uhb[j],), w=(ubb[kc],))
        W1, W2 = self.W["conv_w_pw1"], self.W["conv_w_pw2"]
        CH = CW // 2
        for cg in range(D // CH):
            wa, wab = self.wnext(self.wcols(W1, j, cg * CH, CH), self.wcols(W1, j, D + cg * CH, CH))
            for m in range(CH // P):
                c = cg * (CH // P) + m
                pa, pg = m % 2, 2 + m % 2
                self.proj(pa, wa, wab, m * P, KD, self.h, self.hb, N)
                self.proj(pg, wa, wab, CH + m * P, KD, self.h, self.hb, N)
                s = m % 2
                self.act(sg[s][:, :N], self.ps[pg][:, :N], AF.Sigmoid, r=(self.psb[pg], self.pvb), w=(sgb[s],),
                         bias=self.pvc(f"cb1g{j}", c))
                if not samp:
                    self.stt(ub[:, c, HL:HL + N], self.ps[pa][:, :N], self.pvc(f"cb1a{j}", c), sg[s][:, :N],
                             ALU.add, ALU.mult, r=(self.psb[pa], sgb[s], self.pvb), w=(ubb[c],))
                    if tcx["last"]:
                        self.stt(uf[:, c, :], self.ps[pa][:, N - HL:N], self.pvc(f"cb1a{j}", c),
                                 sg[s][:, N - HL:N], ALU.add, ALU.mult,
                                 r=(self.psb[pa], sgb[s], self.pvb), w=(ufb[c],))
                else:
                    self.stt(uf[:, c, :N], self.ps[pa][:, :N], self.pvc(f"cb1a{j}", c), sg[s][:, :N],
                             ALU.add, ALU.mult, r=(self.psb[pa], sgb[s], self.pvb), w=(ufb[c],))
        sq = [self.wt([P, N], BF16) for _ in range(2)]
        sqb = [self.wbufp("sq") for _ in range(2)]
        ps1, ps2 = self.ps[4], self.ps[5]
        if not samp:
            dg = [self.wt([P, CWID, P], BF16) for _ in range(2)]
            dgb = [self.wbufp("dg") for _ in range(2)]
            for c in range(KD):
                s = c % 2
                self.tt(dg[s][:, :, :], self.identb[:, :].unsqueeze(1).to_broadcast([P, CWID, P]),
                        self.pvc(f"cw{j}", c * CWID, CWID).unsqueeze(2).to_broadcast([P, CWID, P]), ALU.mult,
                        r=(self.pvb, self.cstb), w=(dgb[s],), E=("dve" if c % 2 == 0 else "pool"))
                pc = c % 2
                pairs = [(dg[s][:, tp, :], ub[:, c, tp:tp + N]) for tp in range(CWID)]
                self.mm(self.ps[pc][:, :N], pairs, r=(dgb[s], ubb[c]), w=(self.psb[pc],))
                self.act(cb[:, c, :N], self.ps[pc][:, :N], AF.Identity, r=(self.psb[pc], self.pvb), w=(cbb[c],),
                         bias=self.pvc(f"cbd{j}", c))
                self.act(sq[s][:, :N], self.ps[pc][:, :N], AF.Square, r=(self.psb[pc], self.pvb), w=(sqb[s],),
                         bias=self.pvc(f"cbd{j}", c))
                self.op("pe", (lambda e, c=c: e.matmul(ps1[:, :N], self.onesb[:, :], cb[:, c, :N],
                                                     start=(c == 0), stop=(c == KD - 1))),
                        r=(cbb[c], self.cstb), w=(self.psb[4],))
                self.op("pe", (lambda e, c=c, s=s: e.matmul(ps2[:, :N], self.onesb[:, :], sq[s][:, :N],
                                                          start=(c == 0), stop=(c == KD - 1))),
                        r=(sqb[s], self.cstb), w=(self.psb[5],))
            for kc in range(KD):
                self.cp("dve", self.uh[j][:, kc, :], ub[:, kc, N:N + HL], r=(ubb[kc],), w=(self.uhb[j],))
            if tcx["last"]:
                self.store_tm(self.O["nconv_p"][j], uf, ufb, HL, KD, stq, stqb, key="nconvp")
        else:
            NSg = 4
            st = [self.wt([P, D], F32) for _ in range(2)]
            stb = [self.wbufp("st") for _ in range(2)]
            prod = [self.wt([P, NSg, HL], F32) for _ in range(2)]
            prodb = [self.wbufp("prod") for _ in range(2)]
            red = self.wt([P, KD, N], F32)
            redb = [self.wbufp("red") for _ in range(KD)]
            cs = [self.wt([P, 16], F32) for _ in range(2)]
            csb = [self.wbufp("cs") for _ in range(2)]
            src = self.I["st_conv"][j].rearrange("s j d -> (s j) d")
            R = NSg * HL
            for rg in range(N // NSg):
                s = rg % 2
                self.dma("sp", st[s][:R, :], src[rg * R:(rg + 1) * R, :], r=(), w=(stb[s],), key=("stc", s))
                for c in range(KD):
                    pi = 6 + c % 2
                    self.tr(self.ps[pi][:, :R], st[s][:R, c * P:(c + 1) * P], self.c("ident", R)[:R, :],
                            r=(stb[s], self.cstb), w=(self.psb[pi],))
                    q = c % 2
                    wv = self.pvc(f"cw{j}", c * CWID, HL).unsqueeze(1).to_broadcast([P, NSg, HL])
                    self.tt(prod[q][:, :, :], self.ps[pi][:, :R].rearrange("p (s j) -> p s j", s=NSg), wv, ALU.mult,
                            r=(self.psb[pi], self.pvb), w=(prodb[q],))
                    self.op("dve", (lambda e, q=q, c=c, rg=rg: e.tensor_reduce(
                        red[:, c, rg * NSg:(rg + 1) * NSg], prod[q][:, :, :], AX.X, ALU.add)),
                        r=(prodb[q],), w=(redb[c],))
            for c in range(KD):
                s = c % 2
                self.stt(cs[s][:, :N], uf[:, c, :N], self.pvc(f"cw{j}", c * CWID + HL), red[:, c, :N],
                         ALU.mult, ALU.add, r=(ufb[c], redb[c], self.pvb), w=(csb[s],))
                self.act(cb[:, c, :N], cs[s][:, :N], AF.Identity, r=(csb[s], self.pvb), w=(cbb[c],),
                         bias=self.pvc(f"cbd{j}", c))
                self.act(sq[s][:, :N], cs[s][:, :N], AF.Square, r=(csb[s], self.pvb), w=(sqb[s],),
                         bias=self.pvc(f"cbd{j}", c))
                self.op("pe", (lambda e, c=c: e.matmul(ps1[:, :N], self.onesb[:, :], cb[:, c, :N],
                                                     start=(c == 0), stop=(c == KD - 1))),
                        r=(cbb[c], self.cstb), w=(self.psb[4],))
                self.op("pe", (lambda e, c=c, s=s: e.matmul(ps2[:, :N], self.onesb[:, :], sq[s][:, :N],
                                                          start=(c == 0), stop=(c == KD - 1))),
                        r=(sqb[s], self.cstb), w=(self.psb[5],))
            oc = self.O["nconv_s"][j]
            self.dma("sp", oc[:, 0:HL - 1, :], self.I["st_conv"][j][:, 1:HL, :], key="d2d")
            self.store_tm(oc[:, HL - 1, :], uf, ufb, N, KD, stq, stqb, key="nconvs")
        (mu, mub), (msq, msqb), (var, varb), (rs, rsb), (tmp, tmpb) = scr[2], scr[3], scr[4], scr[5], scr[6]
        t1 = [scr[0][0], scr[1][0]]
        t1b = [scr[0][1], scr[1][1]]
        self.ts(mu[:, :N], ps1[:, :N], 1.0 / D, None, ALU.mult, ALU.bypass, r=(self.psb[4],), w=(mub,))
        self.tt(msq[:, :N], mu[:, :N], mu[:, :N], ALU.mult, r=(mub,), w=(msqb,))
        self.stt(var[:, :N], ps2[:, :N], 1.0 / D, msq[:, :N], ALU.mult, ALU.subtract, r=(self.psb[5], msqb), w=(varb,))
        self.act(tmp[:, :N], var[:, :N], AF.Sqrt, r=(varb, self.cstb), w=(tmpb,), bias=self.c("eps"), scale=1.0)
        self.op("dve", lambda e: e.reciprocal(rs[:, :N], tmp[:, :N]), r=(tmpb,), w=(rsb,))
        for c in range(KD):
            s = c % 2
            self.tt(t1[s][:, :N], cb[:, c, :N], mu[:, :N], ALU.subtract, r=(cbb[c], mub), w=(t1b[s],))
            self.tt(t1[s][:, :N], t1[s][:, :N], rs[:, :N], ALU.mult, r=(t1b[s], rsb), w=(t1b[s],))
            self.act(cb[:, c, :N], t1[s][:, :N], AF.Silu, r=(t1b[s], self.pvb), w=(cbb[c],),
                     bias=self.pvc(f"clb{j}", c), scale=self.pvc(f"clg{j}", c))
        for cg in range(D // CW):
            w2, w2b = self.wnext(self.wcols(W2, j, cg * CW, CW))
            for m in range(MG):
                c = cg * MG + m
                pi = m % 4
                self.proj(pi, w2, w2b, m * P, KD, cb, cbb, N)
                self.stt(x[:, c, :N], self.ps[pi][:, :N], self.pvc(f"cb2{j}", c), x[:, c, :N], ALU.add, ALU.add,
                         r=(self.psb[pi], xb[c], self.pvb), w=(xb[c],))

    def ffn(self, tcx, l):
        cfg = self.cfg
        KD, KF, N, D, FF, MG, CW, KG = cfg.KD, cfg.KF, tcx["N"], cfg.D, cfg.FF, cfg.MG, cfg.CW, cfg.KG
        x, xb = tcx["x"], tcx["xb"]
        samp = tcx["kind"] == "s"
        self.phase()
        self.wreset()
        actb_ = self.wt([P, KF, N], BF16)
        actbb = [self.wbufp("act") for _ in range(KF)]
        Hh = [self.wt([P, 2 + N], F32) for _ in range(4)]
        Hb = [self.wbufp("H") for _ in range(4)]
        tg = [self.wt([P, N], F32) for _ in range(4)]
        tgb = [self.wbufp("tg") for _ in range(4)]
        if samp:
            stf = [self.wt([P, 2, CW], F32) for _ in range(2)]
            stfb = [self.wbufp("stf") for _ in range(2)]
            hn = self.wt([P, 2 * KF, N], F32)
            hnb = [self.wbufp("hn") for _ in range(2 * KF)]
        self.rmsnorm(tcx, f"nf{l}", self.h, self.hb, ((tg[0], tgb[0]), (tg[1], tgb[1])))
        WU, WD = self.W["ffn_w_up"], self.W["ffn_w_down"]
        fw = lambda c, r_: self.pvc(f"fw{l}", c * 3 + r_)
        fbv = lambda c: self.pvc(f"fb{l}", c)
        it = 0
        CH = CW // 2
        for pg in range(FF // CH):
            wg, wgb = self.wnext(self.wcols(WU, l, pg * CH, CH), self.wcols(WU, l, FF + pg * CH, CH))
            if samp:
                ss = pg % 2
                sv = self.I["st_ffn"][l].rearrange("s r f -> (s r) f")
                self.dma("sp", stf[ss][:2 * N, 0, :CH], sv[:, pg * CH:(pg + 1) * CH], r=(), w=(stfb[ss],), key=("stf", ss))
                self.dma("sp", stf[ss][:2 * N, 1, :CH], sv[:, FF + pg * CH:FF + (pg + 1) * CH], r=(), w=(stfb[ss],),
                         key=("stf", ss))
            for m in range(CH // P):
                c = pg * (CH // P) + m
                s = it % 2
                it += 1
                for half, (wt_, wtb_, ci) in enumerate(((wg, wgb, c), (wg, wgb, KF + c))):
                    pi = 2 * half + s
                    hi = 2 * s + half
                    self.proj(pi, wt_, wtb_, half * CH + m * P, KD, self.h, self.hb, N)
                    psv = self.ps[pi]
                    if not samp:
                        Hc = Hh[hi]
                        self.cp("act", Hc[:, 0:2], self.fh[l][:, ci, :], r=(self.fhb[l][ci],), w=(Hb[hi],))
                        self.cp("act", Hc[:, 2:2 + N], psv[:, :N], r=(self.psb[pi],), w=(Hb[hi],))
                        self.cp("act", self.fh[l][:, ci, :], Hc[:, N:N + 2], r=(Hb[hi],), w=(self.fhb[l][ci],))
                        self.act(tg[hi][:, :N], psv[:, :N], AF.Identity, r=(self.psb[pi], self.pvb), w=(tgb[hi],),
                                 bias=fbv(ci), scale=fw(ci, 2))
                        self.stt(tg[hi][:, :N], Hc[:, 1:1 + N], fw(ci, 1), tg[hi][:, :N], ALU.mult, ALU.add,
                                 r=(Hb[hi], tgb[hi], self.pvb), w=(tgb[hi],))
                        self.stt(tg[hi][:, :N], Hc[:, 0:N], fw(ci, 0), tg[hi][:, :N], ALU.mult, ALU.add,
                                 r=(Hb[hi], tgb[hi], self.pvb), w=(tgb[hi],))
                    else:
                        pt = 6 + half
                        self.tr(self.ps[pt][:, :2 * N], stf[ss][:2 * N, half, m * P:(m + 1) * P],
                                self.c("ident", 2 * N)[:2 * N, :], r=(stfb[ss], self.cstb), w=(self.psb[pt],))
                        stv = self.ps[pt][:, :2 * N].rearrange("p (s r) -> p s r", r=2)
                        self.cp("act", hn[:, ci, :N], psv[:, :N], r=(self.psb[pi],), w=(hnb[ci],))
                        self.act(tg[hi][:, :N], psv[:, :N], AF.Identity, r=(self.psb[pi], self.pvb), w=(tgb[hi],),
                                 bias=fbv(ci), scale=fw(ci, 2))
                        self.stt(tg[hi][:, :N], stv[:, :, 1], fw(ci, 1), tg[hi][:, :N], ALU.mult, ALU.add,
                                 r=(self.psb[pt], tgb[hi], self.pvb), w=(tgb[hi],))
                        self.stt(tg[hi][:, :N], stv[:, :, 0], fw(ci, 0), tg[hi][:, :N], ALU.mult, ALU.add,
                                 r=(self.psb[pt], tgb[hi], self.pvb), w=(tgb[hi],))
                g_i, u_i = 2 * s, 2 * s + 1
                self.act(tg[g_i][:, :N], tg[g_i][:, :N], AF.Silu, r=(tgb[g_i],), w=(tgb[g_i],))
                self.tt(actb_[:, c, :N], tg[g_i][:, :N], tg[u_i][:, :N], ALU.mult, r=(tgb[g_i], tgb[u_i]), w=(actbb[c],))
        if not samp and tcx["last"]:
            stg = self.wt([P, 2, P], F32)
            stgb = self.wbufp("stg")
            for r_ in range(2):
                for c0 in range(0, 2 * KF, P):
                    cn = min(P, 2 * KF - c0)
                    self.tr(self.ps[6][:cn, :P], self.fh[l][:, c0:c0 + cn, r_], self.c("ident", P),
                            r=tuple(self.fhb[l][c0:c0 + cn]) + (self.cstb,), w=(self.psb[6],))
                    self.cp("dve", stg[:cn, r_, :], self.ps[6][:cn, :P], r=(self.psb[6],), w=(stgb,))
                    dv = self.O["nffn_p"][l][r_, c0 * P:(c0 + cn) * P].rearrange("(c p) -> c p", p=P)
                    self.dma("sp", dv, stg[:cn, r_, :], r=(stgb,), w=(), key="nffnp")
        if samp:
            of = self.O["nffn_s"][l]
            self.dma("sp", of[:, 0, :], self.I["st_ffn"][l][:, 1, :], key="d2d")
            stg2 = [self.wt([P, 512], F32) for _ in range(2)]
            stg2b = [self.wbufp("stg2") for _ in range(2)]
            for c0 in range(0, 2 * KF, 4):
                cn = min(4, 2 * KF - c0)
                q = (c0 // 4) % 2
                pi = 6 + q
                for k in range(cn):
                    self.tr(self.ps[pi][:N, k * P:(k + 1) * P], hn[:, c0 + k, :N], self.c("ident", P),
                            r=(hnb[c0 + k], self.cstb), w=(self.psb[pi],))
                self.cp("dve", stg2[q][:N, :cn * P], self.ps[pi][:N, :cn * P], r=(self.psb[pi],), w=(stg2b[q],))
                self.dma("sp", of[:, 1, c0 * P:(c0 + cn) * P], stg2[q][:N, :cn * P], r=(stg2b[q],), w=(),
                         key=("nffns", q))
        for mg in range(D // CW):
            for kg in range(KF // KG):
                view = WD[l, kg * KG * P:(kg + 1) * KG * P, mg * CW:(mg + 1) * CW].rearrange("(kc p) n -> p kc n", p=P)
                wd, wdb = self.wnext((view, KG, CW))
                for kc in range(KG):
                    k = kg * KG + kc
                    for jj in range(MG):
                        first = (kg == 0 and kc == 0)
                        last = (kg == KF // KG - 1 and kc == KG - 1)
                        self.op("pe", (lambda e, jj=jj, kc=kc, k=k, first=first, last=last, wd=wd: e.matmul(
                            self.ps[4 + jj][:, :N], wd[:, kc, jj * P:(jj + 1) * P], actb_[:, k, :N],
                            start=first, stop=last)), r=(wdb, actbb[k]), w=(self.psb[4 + jj],))
            for jj in range(MG):
                c = mg * MG + jj
                self.tt(x[:, c, :N], self.ps[4 + jj][:, :N], x[:, c, :N], ALU.add, r=(self.psb[4 + jj], xb[c]), w=(xb[c],))


    def hgrn_prompt(self, tcx, l):
        cfg = self.cfg
        j = l // 2
        KD, N, D, MG, CW, H = cfg.KD, tcx["N"], cfg.D, cfg.MG, cfg.CW, cfg.H
        x, xb = tcx["x"], tcx["xb"]
        self.phase()
        self.wreset()
        NCH = N // GC
        oall = self.wt([P, KD, N], BF16)
        oallb = [self.wbufp("oall") for _ in range(KD)]
        f32t = lambda nm: (self.wt([P, N], F32), self.wbufp(nm))
        bft = lambda nm: (self.wt([P, N], BF16), self.wbufp(nm))
        qf, qfb = f32t("qf")
        ff, ffb = f32t("ff")
        gg, ggb = f32t("gg")
        G, Gb = f32t("G")
        kk, kkb = f32t("kk")
        eG, eGb = gg, ggb
        enG, enGb = ff, ffb
        on, onb = qf, qfb
        rs, rsb = G, Gb
        tmp, tmpb = kk, kkb
        qt2 = [bft("qt"), bft("qt")]
        gth2 = [bft("gth"), bft("gth")]
        kt, ktb = bft("kt")
        vT, vTb = bft("vT")
        khT, khTb = bft("khT")
        osq, osqb = bft("osq")
        vtok = self.wt([P, NCH, P], BF16)
        vtokb = self.wbufp("vtok")
        khtok = self.wt([P, NCH, P], BF16)
        khtokb = self.wbufp("khtok")
        At = self.wt([P, NCH, GC], BF16)
        Atb = self.wbufp("At")
        dec = self.wt([P, 16], F32)
        decb = self.wbufp("dec")
        Ua = self.wt([P, P, NCH], F32)
        Uab = self.wbufp("Ua")
        Sa, Sab = Ua, Uab
        drep = self.wt([P, P, NCH], F32)
        drepb = self.wbufp("drep")
        Sab16 = self.wt([P, NCH, P], BF16)
        Sab16b = self.wbufp("Sab16")
        self.rmsnorm(tcx, f"nm{l}", self.h, self.hb, ((rs, rsb), (tmp, tmpb)))
        Wq, Wf, Wi, Wg, Wo = (self.W[k] for k in ("hgrn_w_q", "hgrn_w_f", "hgrn_w_i", "hgrn_w_g", "hgrn_w_o"))

        def proj_head(hd):
            wq, wqb = self.wnext(self.wcols(Wq, j, hd * P, P), self.wcols(Wf, j, hd * P, P),
                                 self.wcols(Wi, j, hd * P, P), self.wcols(Wg, j, hd * P, P))
            for i in (1, 2, 0, 3):
                self.proj(i, wq, wqb, i * P, KD, self.h, self.hb, N)

        def prep_head(hd):
            qt, qtb = qt2[hd % 2]
            gth, gthb = gth2[hd % 2]
            self.act(ff[:, :N], self.ps[1][:, :N], AF.Sigmoid, r=(self.psb[1],), w=(ffb,))
            self.cp("dve", vT[:, :N], self.ps[2][:, :N], r=(self.psb[2],), w=(vTb,))
            self.ts(ff[:, :N], ff[:, :N], self.omlv[:, j, hd:hd + 1], self.lbv[:, j, hd:hd + 1], ALU.mult, ALU.add,
                    r=(ffb, self.lbb), w=(ffb,))
            self.ts(kk[:, :N], ff[:, :N], -1.0, 1.0, ALU.mult, ALU.add, r=(ffb,), w=(kkb,))
            self.act(gg[:, :N], ff[:, :N], AF.Ln, r=(ffb,), w=(ggb,))
            self.op("dve", lambda e: e.tensor_tensor_scan(G[:, :N], self.c("rmask", N), gg[:, :N], 0.0,
                                                            ALU.mult, ALU.add), r=(ggb, self.cstb), w=(Gb,))
            self.act(enG[:, :N], G[:, :N], AF.Exp, r=(Gb,), w=(enGb,), scale=-1.0)
            self.act(dec[:, :NCH], G[:, GC - 1:N:GC], AF.Exp, r=(Gb,), w=(decb,))
            self.tt(kt[:, :N], kk[:, :N], enG[:, :N], ALU.mult, r=(kkb, enGb), w=(ktb,))
            self.tt(khT[:, :N].rearrange("p (c t) -> p c t", t=GC), kt[:, :N].rearrange("p (c t) -> p c t", t=GC),
                    dec[:, :NCH].unsqueeze(2).to_broadcast([P, NCH, GC]), ALU.mult, r=(ktb, decb), w=(khTb,))
            self.act(eG[:, :N], G[:, :N], AF.Exp, r=(Gb,), w=(eGb,))
            self.act(drep[:, :, :], G[:, GC - 1:N:GC].unsqueeze(1).to_broadcast([P, P, NCH]), AF.Exp, r=(Gb,), w=(drepb,))
            self.op("dve", lambda e: e.memset(drep[:, :, 0:1], 0.0), r=(), w=(drepb,))
            self.act(qf[:, :N], self.ps[0][:, :N], AF.Silu, r=(self.psb[0],), w=(qfb,))
            self.act(gth[:, :N], self.ps[3][:, :N], AF.Silu, r=(self.psb[3],), w=(gthb,))
            self.tt(qt[:, :N], qf[:, :N], eG[:, :N], ALU.mult, r=(qfb, eGb), w=(qtb,))

        def gla_A(hd):
            qt, qtb = qt2[hd % 2]
            Sfh, Sbh = self.Sf[j][:, hd, :], self.Sb[j][:, hd, :]
            Sfhb, Sbhb = self.Sfb[j][hd], self.Sbb[j][hd]
            p7 = self.ps[7][:, :].bitcast(BF16)
            for src, srcb_, dst, dstb_ in ((vT, vTb, vtok, vtokb), (khT, khTb, khtok, khtokb)):
                for c0 in range(0, NCH, 8):
                    cn = min(8, NCH - c0)
                    for k in range(cn):
                        ch = c0 + k
                        self.tr(p7[:GC, k * P:(k + 1) * P], src[:, ch * GC:(ch + 1) * GC], self.identb[:, :],
                                r=(srcb_, self.cstb), w=(self.psb[7],))
                    self.cp("act", dst[:GC, c0:c0 + cn, :], p7[:GC, :cn * P].rearrange("p (c k) -> p c k", k=P),
                            r=(self.psb[7],), w=(dstb_,))
            for c0 in range(0, NCH, 4):
                cn = min(4, NCH - c0)
                pi = 4 + (c0 // 4) % 2
                for k in range(cn):
                    ch = c0 + k
                    self.mm(self.ps[pi][:, k * P:(k + 1) * P], [(khtok[:GC, ch, :], vtok[:GC, ch, :])],
                            r=(khtokb, vtokb), w=(self.psb[pi],))
                self.cp("act", Ua[:, :, c0:c0 + cn].rearrange("p v c -> p c v"),
                        self.ps[pi][:, :cn * P].rearrange("p (c v) -> p c v", v=P), r=(self.psb[pi],), w=(Uab,))
            for ch in range(NCH):
                self.mm(self.ps[7][:GC, ch * GC:(ch + 1) * GC],
                        [(kt[:, ch * GC:(ch + 1) * GC], qt[:, ch * GC:(ch + 1) * GC])],
                        r=(ktb, qtb), w=(self.psb[7],))
            self.tt(At[:GC, :, :], self.ps[7][:GC, :NCH * GC].rearrange("p (c t) -> p c t", t=GC),
                    self.c("triu", GC)[:GC, :].unsqueeze(1).to_broadcast([GC, NCH, GC]), ALU.mult,
                    r=(self.psb[7], self.cstb), w=(Atb,))
            self.stt(Ua[:, :, 0], Sfh, dec[:, 0:1], Ua[:, :, 0], ALU.mult, ALU.add, r=(Sfhb, decb, Uab), w=(Uab,))
            self.op("dve", lambda e: e.tensor_tensor_scan(Sa[:, :, :].rearrange("p v c -> p (v c)"),
                                                            drep[:, :, :].rearrange("p v c -> p (v c)"),
                                                            Ua[:, :, :].rearrange("p v c -> p (v c)"), 0.0,
                                                            ALU.mult, ALU.add), r=(Uab, drepb), w=(Uab,))
            self.cp("act", Sab16[:, :, :], Sa[:, :, :].rearrange("p v c -> p c v"), r=(Sab,), w=(Sab16b,))

        def gla_B(hd):
            qt, qtb = qt2[hd % 2]
            Sfh, Sbh = self.Sf[j][:, hd, :], self.Sb[j][:, hd, :]
            Sfhb, Sbhb = self.Sfb[j][hd], self.Sbb[j][hd]
            for ch in range(NCH):
                cs_ = slice(ch * GC, (ch + 1) * GC)
                lhs = Sbh if ch == 0 else Sab16[:, ch - 1, :]
                self.mm(self.ps[6][:, cs_], [(lhs, qt[:, cs_]), (vtok[:GC, ch, :], At[:GC, ch, :])],
                        r=(Sbhb, Sab16b, qtb, vtokb, Atb), w=(self.psb[6],))
            self.cp("dve", Sfh, Sa[:, :, NCH - 1], r=(Sab,), w=(Sfhb,))
            self.cp("act", Sbh, Sab16[:, NCH - 1, :], r=(Sab16b,), w=(Sbhb,))

        def norm_head(hd):
            gth, gthb = gth2[hd % 2]
            pso, psob = self.ps[6], self.psb[6]
            self.act(osq[:, :N], pso[:, :N], AF.Square, r=(psob,), w=(osqb,))
            self.mm(self.ps[5][:, :N], [(self.onesb[:, :], osq[:, :N])], r=(osqb, self.cstb), w=(self.psb[5],))
            self.rstd_from(self.ps[5][:, :N], self.psb[5], N, 1.0 / P, rs[:, :N], rsb, tmp[:, :N], tmpb)
            self.stt(on[:, :N], pso[:, :N], self.pvc(f"hng{j}", hd), rs[:, :N], ALU.mult, ALU.mult,
                     r=(psob, rsb, self.pvb), w=(onb,))
            self.tt(oall[:, hd, :N], on[:, :N], gth[:, :N], ALU.mult, r=(onb, gthb), w=(oallb[hd],))

        proj_head(0)
        prep_head(0)
        gla_A(0)
        for hd in range(1, H):
            proj_head(hd)
            gla_B(hd - 1)
            prep_head(hd)
            norm_head(hd - 1)
            gla_A(hd)
        gla_B(H - 1)
        norm_head(H - 1)
        if tcx["last"]:
            ov = self.O["nhgrn_p"][j].rearrange("h k v -> k h v")
            self.dma("sp", ov, self.Sf[j][:, :, :], r=tuple(self.Sfb[j]), w=(), key="nhgrnp")
        for cg in range(D // CW):
            wo, wob = self.wnext(self.wcols(Wo, j, cg * CW, CW))
            for m in range(MG):
                c = cg * MG + m
                pi = m % 4
                self.proj(pi, wo, wob, m * P, KD, oall, oallb, N)
                self.tt(x[:, c, :N], self.ps[pi][:, :N], x[:, c, :N], ALU.add, r=(self.psb[pi], xb[c]), w=(xb[c],))

    def hgrn(self, tcx, l):
        cfg = self.cfg
        j = l // 2
        KD, N, D, MG, CW, H = cfg.KD, tcx["N"], cfg.D, cfg.MG, cfg.CW, cfg.H
        x, xb = tcx["x"], tcx["xb"]
        samp = tcx["kind"] == "s"
        self.phase()
        self.wreset()
        T = cfg.T
        gate = self.wt([P, KD, N], BF16)
        gateb = [self.wbufp("gate") for _ in range(KD)]
        oall = self.wt([P, KD, N], BF16)
        oallb = [self.wbufp("oall") for _ in range(KD)]
        f32t = lambda nm: (self.wt([P, N], F32), self.wbufp(nm))
        qf, qfb = f32t("qf")
        ff, ffb = f32t("ff")
        gg, ggb = f32t("gg")
        G, Gb = f32t("G")
        kk, kkb = f32t("kk")
        eG, eGb = gg, ggb
        enG, enGb = ff, ffb
        on, onb = f32t("on")
        rs, rsb = f32t("rs")
        tmp, tmpb = f32t("tmp")
        bft = lambda nm: (self.wt([P, N], BF16), self.wbufp(nm))
        qt, qtb = bft("qt")
        kt, ktb = bft("kt")
        vT, vTb = bft("vT")
        osq, osqb = bft("osq")
        if not samp:
            khT, khTb = bft("khT")
            NCH = N // GC
            vtok = self.wt([P, NCH, P], BF16)
            vtokb = self.wbufp("vtok")
            khtok = self.wt([P, NCH, P], BF16)
            khtokb = self.wbufp("khtok")
            At = self.wt([P, NCH, GC], BF16)
            Atb = self.wbufp("At")
            dec = self.wt([P, max(16, NCH)], F32)
            decb = self.wbufp("dec")
        else:
            ktok = self.wt([P, P], BF16)
            ktokb = self.wbufp("ktok")
            vtk = self.wt([P, P], BF16)
            vtkb = self.wbufp("vtk")
            vblk = self.wt([P, N, P], BF16)
            vblkb = self.wbufp("vblk")
            Sin = [self.wt([P, N, P], F32) for _ in range(2)]
            Sinb = [self.wbufp("Sin") for _ in range(2)]
            Sn = self.wt([P, N, P], F32)
            Snb = self.wbufp("Sn")
            Snbf = self.wt([P, N, P], BF16)
            Snbfb = self.wbufp("Snbf")
            qb16 = self.wt([P, 16], BF16)
            qb16b = self.wbufp("qb16")
        self.rmsnorm(tcx, f"nm{l}", self.h, self.hb, ((rs, rsb), (tmp, tmpb)))
        Wq, Wf, Wi, Wg, Wo = (self.W[k] for k in ("hgrn_w_q", "hgrn_w_f", "hgrn_w_i", "hgrn_w_g", "hgrn_w_o"))
        for hd in range(H):
            wq, wqb = self.wnext(self.wcols(Wq, j, hd * P, P), self.wcols(Wf, j, hd * P, P),
                                 self.wcols(Wi, j, hd * P, P), self.wcols(Wg, j, hd * P, P))
            for m in range(1):
                if samp:
                    s2 = hd % 2
                    sv = self.I["st_hgrn"][j][:, hd].rearrange("s k v -> k s v")
                    self.dma("sp", Sin[s2][:, :, :], sv, r=(), w=(Sinb[s2],), key=("Sin", s2))
                self.proj(0, wq, wqb, 0, KD, self.h, self.hb, N)
                self.proj(1, wq, wqb, P, KD, self.h, self.hb, N)
                self.proj(2, wq, wqb, 2 * P, KD, self.h, self.hb, N)
                self.proj(3, wq, wqb, 3 * P, KD, self.h, self.hb, N)
                self.act(qf[:, :N], self.ps[0][:, :N], AF.Silu, r=(self.psb[0],), w=(qfb,))
                self.act(gate[:, hd, :N], self.ps[3][:, :N], AF.Silu, r=(self.psb[3],), w=(gateb[hd],))
                self.act(ff[:, :N], self.ps[1][:, :N], AF.Sigmoid, r=(self.psb[1],), w=(ffb,))
                self.cp("dve", vT[:, :N], self.ps[2][:, :N], r=(self.psb[2],), w=(vTb,))
                self.ts(ff[:, :N], ff[:, :N], self.omlv[:, j, hd:hd + 1], self.lbv[:, j, hd:hd + 1], ALU.mult, ALU.add,
                        r=(ffb, self.lbb), w=(ffb,))
                self.ts(kk[:, :N], ff[:, :N], -1.0, 1.0, ALU.mult, ALU.add, r=(ffb,), w=(kkb,))
                if not samp:
                    self.act(gg[:, :N], ff[:, :N], AF.Ln, r=(ffb,), w=(ggb,))
                    self.op("dve", lambda e: e.tensor_tensor_scan(G[:, :N], self.c("rmask", N), gg[:, :N], 0.0,
                                                                    ALU.mult, ALU.add),
                            r=(ggb, self.cstb), w=(Gb,))
                    self.act(eG[:, :N], G[:, :N], AF.Exp, r=(Gb,), w=(eGb,))
                    self.act(enG[:, :N], G[:, :N], AF.Exp, r=(Gb,), w=(enGb,), scale=-1.0)
                    Glast = G[:, GC - 1:N:GC]
                    self.act(dec[:, :NCH], Glast, AF.Exp, r=(Gb,), w=(decb,))
                    self.tt(qt[:, :N], qf[:, :N], eG[:, :N], ALU.mult, r=(qfb, eGb), w=(qtb,))
                    self.tt(kt[:, :N], kk[:, :N], enG[:, :N], ALU.mult, r=(kkb, enGb), w=(ktb,))
                    self.tt(khT[:, :N].rearrange("p (c t) -> p c t", t=GC), kt[:, :N].rearrange("p (c t) -> p c t", t=GC),
                            dec[:, :NCH].unsqueeze(2).to_broadcast([P, NCH, GC]), ALU.mult, r=(ktb, decb), w=(khTb,))
                    p7 = self.ps[7][:, :].bitcast(BF16)
                    for src, srcb_, dst, dstb_ in ((vT, vTb, vtok, vtokb), (khT, khTb, khtok, khtokb)):
                        for c0 in range(0, NCH, 8):
                            cn = min(8, NCH - c0)
                            for k in range(cn):
                                ch = c0 + k
                                self.tr(p7[:GC, k * P:(k + 1) * P], src[:, ch * GC:(ch + 1) * GC], self.identb[:, :],
                                        r=(srcb_, self.cstb), w=(self.psb[7],))
                            self.cp("act", dst[:GC, c0:c0 + cn, :], p7[:GC, :cn * P].rearrange("p (c k) -> p c k", k=P),
                                    r=(self.psb[7],), w=(dstb_,))
                    for ch in range(NCH):
                        self.mm(self.ps[5][:GC, ch * GC:(ch + 1) * GC],
                                [(kt[:, ch * GC:(ch + 1) * GC], qt[:, ch * GC:(ch + 1) * GC])],
                                r=(ktb, qtb), w=(self.psb[5],))
                    self.tt(At[:GC, :, :], self.ps[5][:GC, :NCH * GC].rearrange("p (c t) -> p c t", t=GC),
                            self.c("triu", GC)[:GC, :].unsqueeze(1).to_broadcast([GC, NCH, GC]), ALU.mult,
                            r=(self.psb[5], self.cstb), w=(Atb,))
                    Sfh, Sbh = self.Sf[j][:, hd, :], self.Sb[j][:, hd, :]
                    Sfhb, Sbhb = self.Sfb[j][hd], self.Sbb[j][hd]
                    for ch in range(NCH):
                        cs_ = slice(ch * GC, (ch + 1) * GC)
                        self.mm(self.ps[6][:, cs_], [(Sbh, qt[:, cs_]), (vtok[:GC, ch, :], At[:GC, ch, :])],
                                r=(Sbhb, qtb, vtokb, Atb), w=(self.psb[6],))
                        self.mm(self.ps[4][:, :P], [(khtok[:GC, ch, :], vtok[:GC, ch, :])], r=(khtokb, vtokb),
                                w=(self.psb[4],))
                        self.stt(Sfh, Sfh, dec[:, ch:ch + 1], self.ps[4][:, :P], ALU.mult, ALU.add,
                                 r=(Sfhb, decb, self.psb[4]), w=(Sfhb,))
                        self.cp("act", Sbh, Sfh, r=(Sfhb,), w=(Sbhb,))
                    pso, psob = self.ps[6], self.psb[6]
                else:
                    self.cp("act", kt[:, :N], kk[:, :N], r=(kkb,), w=(ktb,))
                    self.cp("act", qb16[:, :N], qf[:, :N], r=(qfb,), w=(qb16b,))
                    p7 = self.ps[7][:, :].bitcast(BF16)
                    self.tr(p7[:N, 0:P], kt[:, :N], self.identb[:, :], r=(ktb, self.cstb), w=(self.psb[7],))
                    self.tr(p7[:N, P:2 * P], vT[:, :N], self.identb[:, :], r=(vTb, self.cstb), w=(self.psb[7],))
                    self.cp("act", ktok[:N, :], p7[:N, 0:P], r=(self.psb[7],), w=(ktokb,))
                    self.cp("act", vtk[:N, :], p7[:N, P:2 * P], r=(self.psb[7],), w=(vtkb,))
                    self.tt(vblk[:N, :, :], vtk[:N, :].unsqueeze(1).to_broadcast([N, N, P]),
                            self.c("ident", N)[:N, :].unsqueeze(2).to_broadcast([N, N, P]), ALU.mult,
                            r=(vtkb, self.cstb), w=(vblkb,))
                    for q4 in range(N // 4):
                        self.mm(self.ps[4 + q4 % 2][:, :],
                                [(ktok[:N, :], vblk[:N, q4 * 4:(q4 + 1) * 4, :].rearrange("p s v -> p (s v)"))],
                                r=(ktokb, vblkb), w=(self.psb[4 + q4 % 2],))
                        sl = slice(q4 * 4, (q4 + 1) * 4)
                        self.tt(Sn[:, sl, :], Sin[s2][:, sl, :], ff[:, sl].unsqueeze(2).to_broadcast([P, 4, P]), ALU.mult,
                                r=(Sinb[s2], ffb), w=(Snb,))
                        self.tt(Sn[:, sl, :], Sn[:, sl, :], self.ps[4 + q4 % 2][:, :].rearrange("p (s v) -> p s v", v=P),
                                ALU.add, r=(Snb, self.psb[4 + q4 % 2]), w=(Snb,))
                    self.cp("act", Snbf[:, :, :], Sn[:, :, :], r=(Snb,), w=(Snbfb,))
                    ov = self.O["nhgrn_s"][j][:, hd].rearrange("s k v -> k s v")
                    self.dma("sp", ov, Sn[:, :, :], r=(Snb,), w=(), key="Snout")
                    for s_ in range(N):
                        self.mm(self.ps[6][:, s_:s_ + 1], [(Snbf[:, s_, :], qb16[:, s_:s_ + 1])], r=(Snbfb, qb16b),
                                w=(self.psb[6],))
                    pso, psob = self.ps[6], self.psb[6]
                self.act(osq[:, :N], pso[:, :N], AF.Square, r=(psob,), w=(osqb,))
                self.mm(self.ps[5][:, :N], [(self.onesb[:, :], osq[:, :N])], r=(osqb, self.cstb), w=(self.psb[5],))
                self.rstd_from(self.ps[5][:, :N], self.psb[5], N, 1.0 / P, rs[:, :N], rsb, tmp[:, :N], tmpb)
                self.stt(on[:, :N], pso[:, :N], self.pvc(f"hng{j}", hd), rs[:, :N], ALU.mult, ALU.mult,
                         r=(psob, rsb, self.pvb), w=(onb,))
                self.tt(oall[:, hd, :N], on[:, :N], gate[:, hd, :N], ALU.mult, r=(onb, gateb[hd]), w=(oallb[hd],))
        if not samp and tcx["last"]:
            ov = self.O["nhgrn_p"][j].rearrange("h k v -> k h v")
            self.dma("sp", ov, self.Sf[j][:, :, :], r=tuple(self.Sfb[j]), w=(), key="nhgrnp")
        for cg in range(D // CW):
            wo, wob = self.wnext(self.wcols(Wo, j, cg * CW, CW))
            for m in range(MG):
                c = cg * MG + m
                pi = m % 4
                self.proj(pi, wo, wob, m * P, KD, oall, oallb, N)
                self.tt(x[:, c, :N], self.ps[pi][:, :N], x[:, c, :N], ALU.add, r=(self.psb[pi], xb[c]), w=(xb[c],))

    def emit(self):
        cfg = self.cfg
        KD, T, NS, D, NTL = cfg.KD, cfg.T, cfg.NS, cfg.D, cfg.NTL
        if not self.planning:
            self.pfx = ""
            self.dma("sp", self.pv[:, :], self.I["pv"], w=(self.pvb,), key="pv")
            self.dma("sp", self.cst[:, :], self.I["cst"], w=(self.cstb,), key="cst")
            self.op("dve", lambda e: e.tensor_copy(self.identb[:, :], self.c("ident", P)), r=(self.cstb,), w=(self.cstb,))
            self.op("dve", lambda e: e.memset(self.onesb[:, :], 1.0), r=(), w=(self.cstb,))
            assert cfg.NHL == 2
            self.op("dve", lambda e: e.memset(self.lbvS[:, 0, :], 0.0), r=(), w=(self.lbb,))
            self.tt(self.lbvS[:, 1, :], self.pvc("hraw1", 0, KD), self.pvc("hraw0", 0, KD), ALU.subtract,
                    r=(self.pvb,), w=(self.lbb,))
            self.act(self.lbvS[:, 1, :], self.lbvS[:, 1, :], AF.Sigmoid, r=(self.lbb,), w=(self.lbb,))
            self.ts(self.lbvP[:, 0, :], self.lbvS[:, 1, :], self.pvc("mB", 0, 1), None, ALU.mult, ALU.bypass,
                    r=(self.lbb, self.pvb), w=(self.lbb,))
            self.ts(self.omlvS[:, :, :], self.lbvS[:, :, :], -1.0, 1.0, ALU.mult, ALU.add, r=(self.lbb,), w=(self.lbb,))
            self.ts(self.omlvP[:, :, :], self.lbvP[:, :, :], -1.0, 1.0, ALU.mult, ALU.add, r=(self.lbb,), w=(self.lbb,))
            self.op("dve", lambda e: e.memset(self.Sf[0][:, :, :], 0.0), w=tuple(self.Sfb[0]))
            self.op("dve", lambda e: e.memset(self.Sb[0][:, :, :], 0.0), w=tuple(self.Sbb[0]))
            self.op("dve", lambda e: e.memset(self.uh[0][:, :, :], 0.0), w=(self.uhb[0],))
            for l in range(2):
                self.op("dve", lambda e, l=l: e.memset(self.fh[l][:, :, :], 0.0), w=tuple(self.fhb[l]))
        NSTEP = NTL + 1
        for s in range(NSTEP):
            ti = min(s, NTL - 1)
            oslot = 1 if s == NTL else 0
            tcx = dict(kind="p", N=T, ti=ti, last=(s >= NTL - 1), x=self.x, xb=self.xb)
            self.set_mode("P", oslot)
            self.phase()
            self.wreset()
            stg = [self.wt([P, D], F32) for _ in range(2)]
            stgb = [self.wbufp("xin") for _ in range(2)]
            for b in range(T // P):
                rows = self.I["xp"][ti * T + b * P:ti * T + (b + 1) * P, :]
                self.load_fm(self.x, self.xb, rows, P, KD, stg[b % 2], stgb[b % 2], key=("xin", b % 2), dst_col0=b * P)
            if s >= 1:
                xin2 = self.wt([P, KD, T], F32)
                xin2b = self.wbufp("xin2")
                KQ = KD // self.NQ
                for q in range(self.NQ):
                    g = self.xg[(s - 1) % 2][q]
                    self.dma("sp", xin2[:, q * KQ:(q + 1) * KQ, :], g[0:P, :].rearrange("p (k t) -> p k t", k=KQ),
                             r=(self.xgb[(s - 1) % 2][q],), w=(xin2b,), key="xin2")
                for kc in range(KD):
                    self.ts(self.x[:, kc, :], self.x[:, kc, :], self.pvc("mA", 0, 1), None, ALU.mult, ALU.bypass,
                            r=(self.xb[kc], self.pvb), w=(self.xb[kc],))
                    self.stt(self.x[:, kc, :], xin2[:, kc, :], self.pvc("mB", 0, 1), self.x[:, kc, :], ALU.mult, ALU.add,
                             r=(xin2b, self.xb[kc], self.pvb), w=(self.xb[kc],))
            self.conv_mixer(tcx, 0)
            self.ffn(tcx, 0)
            if getattr(cfg, "OLDHGRN", False):
                self.hgrn(tcx, 1)
            else:
                self.hgrn_prompt(tcx, 1)
            self.ffn(tcx, 1)
            if s == 0:
                mA = self.pvc("mA", 0, 1)
                self.ts(self.uh[0][:, :, :], self.uh[0][:, :, :], mA, None, ALU.mult, ALU.bypass,
                        r=(self.uhb[0], self.pvb), w=(self.uhb[0],))
                self.ts(self.Sf[0][:, :, :], self.Sf[0][:, :, :], mA, None, ALU.mult, ALU.bypass,
                        r=tuple(self.Sfb[0]) + (self.pvb,), w=tuple(self.Sfb[0]))
                self.ts(self.Sb[0][:, :, :], self.Sb[0][:, :, :], mA, None, ALU.mult, ALU.bypass,
                        r=tuple(self.Sbb[0]) + (self.pvb,), w=tuple(self.Sbb[0]))
                for l in range(2):
                    self.ts(self.fh[l][:, :, :], self.fh[l][:, :, :], mA, None, ALU.mult, ALU.bypass,
                            r=tuple(self.fhb[l]) + (self.pvb,), w=tuple(self.fhb[l]))
            if s < NSTEP - 1:
                KQ = KD // self.NQ
                for q in range(self.NQ):
                    self.dma("sp", self.xo[q][:, :].rearrange("p (k t) -> p k t", k=KQ), self.x[:, q * KQ:(q + 1) * KQ, :],
                             r=tuple(self.xb[q * KQ:(q + 1) * KQ]), w=(self.xob[q],), key=("xo", q))
                    self.coll_allgather(self.xo[q].opt(), self.xg[s % 2][q].opt(), r=(self.xob[q],),
                                        w=(self.xgb[s % 2][q],))
            if s >= 1:
                self.phase()
                self.wreset()
                yn = self.wt([P, KD, T], F32)
                ynb = [self.wbufp("yn") for _ in range(KD)]
                stg = [self.wt([P, 512], F32) for _ in range(2)]
                stgb = [self.wbufp("yout") for _ in range(2)]
                scr = ((self.wt([P, T], F32), self.wbufp("scr")), (self.wt([P, T], F32), self.wbufp("scr")))
                self.rmsnorm(tcx, "nfin", yn, ynb, scr)
                for b in range(T // P):
                    rows = self.O["y"][(s - 1) * T + b * P:(s - 1) * T + (b + 1) * P, :]
                    self.store_tm(rows, yn, ynb, P, KD, stg, stgb, key="yout", src_col0=b * P)
        self.set_mode("S")
        tcx = dict(kind="s", N=NS, ti=0, last=True, x=self.x, xb=self.xb)
        self.phase()
        self.wreset()
        stg = [self.wt([P, D], F32) for _ in range(2)]
        stgb = [self.wbufp("xin") for _ in range(2)]
        self.load_fm(self.x, self.xb, self.I["xs"], NS, KD, stg[0], stgb[0], key=("xin", 0))
        for l in range(cfg.DEPTH):
            if l % 2 == 0:
                self.conv_mixer(tcx, l)
            else:
                self.hgrn(tcx, l)
            self.ffn(tcx, l)
        self.phase()
        self.wreset()
        yn = self.wt([P, KD, NS], F32)
        ynb = [self.wbufp("yn") for _ in range(KD)]
        stg = [self.wt([P, 512], F32) for _ in range(2)]
        stgb = [self.wbufp("yout") for _ in range(2)]
        scr = ((self.wt([P, NS], F32), self.wbufp("scr")), (self.wt([P, NS], F32), self.wbufp("scr")))
        self.rmsnorm(tcx, "nfin", yn, ynb, scr)
        self.store_tm(self.O["ys"], yn, ynb, NS, KD, stg, stgb, key="yout")
        if not self.planning:
            sp = self.eng["sp"]
            for key, (k, v) in self.dsems.items():
                if sp.seen.get(k, 0) < v:
                    sp.obj.wait_ge(self.sems[k], v)
                    sp.seen[k] = v
            for en in ("pe", "act", "dve", "pool"):
                e = self.eng[en]
                if e.sem is not None and e.count > 0:
                    sp.obj.wait_ge(self.sems[e.sem], e.count)

    def build(self):
        cfg, nc = self.cfg, self.nc
        D, FF, NS, KD, T = cfg.D, cfg.FF, cfg.NS, cfg.KD, cfg.T
        di = lambda n, s: nc.dram_tensor(n, list(s), F32, kind="ExternalInput").ap()
        do = lambda n, s: nc.dram_tensor(n, list(s), F32, kind="ExternalOutput").ap()
        self.I = {
            "xp": di("xp", [cfg.SEQ, D]), "xs": di("xs", [NS, D]),
            "st_conv": di("st_conv", [cfg.NCL, NS, CWID - 1, D]),
            "st_hgrn": di("st_hgrn", [cfg.NHL, NS, cfg.H, P, P]),
            "st_ffn": di("st_ffn", [cfg.DEPTH, NS, 2, 2 * FF]),
            "pv": di("pv", [P, self.npv]), "cst": di("cst", [P, self.ncst]),
        }
        self.WS = {
            "conv_w_pw1": di("conv_w_pw1", [cfg.NCL, D, 2 * D]), "conv_w_pw2": di("conv_w_pw2", [cfg.NCL, D, D]),
            "hgrn_w_q": di("hgrn_w_q", [cfg.NHL, D, D]), "hgrn_w_f": di("hgrn_w_f", [cfg.NHL, D, D]),
            "hgrn_w_i": di("hgrn_w_i", [cfg.NHL, D, D]), "hgrn_w_g": di("hgrn_w_g", [cfg.NHL, D, D]),
            "hgrn_w_o": di("hgrn_w_o", [cfg.NHL, D, D]),
            "ffn_w_up": di("ffn_w_up", [cfg.DEPTH, D, 2 * FF]), "ffn_w_down": di("ffn_w_down", [cfg.DEPTH, FF, D]),
        }
        self.WP = {
            "conv_w_pw1": di("p_conv_w_pw1", [1, D, 2 * D]), "conv_w_pw2": di("p_conv_w_pw2", [1, D, D]),
            "hgrn_w_q": di("p_hgrn_w_q", [1, D, D]), "hgrn_w_f": di("p_hgrn_w_f", [1, D, D]),
            "hgrn_w_i": di("p_hgrn_w_i", [1, D, D]), "hgrn_w_g": di("p_hgrn_w_g", [1, D, D]),
            "hgrn_w_o": di("p_hgrn_w_o", [1, D, D]),
            "ffn_w_up": di("p_ffn_w_up", [2, D, 2 * FF]), "ffn_w_down": di("p_ffn_w_down", [2, FF, D]),
        }
        self.W = self.WS
        self.Oall = {
            "nconv_p": do("nconv_p", [2, CWID - 1, D]), "nhgrn_p": do("nhgrn_p", [2, cfg.H, P, P]),
            "nffn_p": do("nffn_p", [2, 2, 2, 2 * FF]),
        }
        self.O = {
            "y": do("y", [cfg.SEQ, D]), "ys": do("ys", [NS, D]),
            "nconv_s": do("nconv_s", [cfg.NCL, NS, CWID - 1, D]), "nhgrn_s": do("nhgrn_s", [cfg.NHL, NS, cfg.H, P, P]),
            "nffn_s": do("nffn_s", [cfg.DEPTH, NS, 2, 2 * FF]),
        }
        NQ = min(4, KD)
        KQ = KD // NQ
        self.xo = [nc.dram_tensor(f"xo{q}", [P, KQ * T], F32).ap() for q in range(NQ)]
        self.xg = [[nc.dram_tensor(f"xg{i}_{q}", [2 * P, KQ * T], F32).ap() for q in range(NQ)] for i in range(2)]
        self.alloc()
        self.planning = True
        self.plan = []
        self.emit()
        self.planning = False
        self.wi = 0
        self.wissued = 0
        self.woff = 0
        self.emit()
        assert self.wi == len(self.plan), (self.wi, len(self.plan))
        return nc


def run(cfg, inp, n_cores=8, trace=False):
    D, FF = cfg.D, cfg.FF
    pvs = [pack_params(cfg, inp, c) for c in range(n_cores)]
    pv0 = pvs[0].build()
    coff, cst = make_consts(cfg)
    b = Builder(cfg, pvs[0].off, pv0.shape[1], coff, cst.shape[1])
    nc = b.build()
    B = inp["x_prompt"].shape[0]
    assert n_cores == 2 * B
    NS = cfg.NS
    wnames = ["conv_w_pw1", "conv_w_pw2", "hgrn_w_q", "hgrn_w_f", "hgrn_w_i", "hgrn_w_g", "hgrn_w_o", "ffn_w_up", "ffn_w_down"]
    in_maps = []
    for c in range(n_cores):
        isB = c % 2
        m = {
            "xp": np.ascontiguousarray(inp["x_prompt"][c // 2]),
            "xs": np.ascontiguousarray(inp["x_sample"][c * NS:(c + 1) * NS, 0, :]),
            "st_conv": np.ascontiguousarray(inp["state_conv"][:, c * NS:(c + 1) * NS]),
            "st_hgrn": np.ascontiguousarray(inp["state_hgrn"][:, c * NS:(c + 1) * NS]),
            "st_ffn": np.ascontiguousarray(inp["state_ffn"][:, c * NS:(c + 1) * NS]),
            "pv": pvs[c].build(), "cst": cst,
        }
        for w in wnames:
            m[w] = inp[w]
            if w.startswith("ffn"):
                m["p_" + w] = np.ascontiguousarray(inp[w][2 * isB:2 * isB + 2])
            else:
                m["p_" + w] = np.ascontiguousarray(inp[w][isB:isB + 1])
        in_maps.append(m)
    res = run_bass_kernel_spmd(nc, in_maps, core_ids=list(range(n_cores)), trace=trace)
    R = res.results
    y_prompt = np.stack([R[2 * b + 1]["y"] for b in range(B)], axis=0)
    y_sample = np.concatenate([R[c]["ys"] for c in range(n_cores)], axis=0)[:, None, :]
    conv_p = np.stack([np.stack([R[2 * b + j]["nconv_p"][j] for b in range(B)], axis=0) for j in range(2)], axis=0)
    hgrn_p = np.stack([np.stack([R[2 * b + j]["nhgrn_p"][j] for b in range(B)], axis=0) for j in range(2)], axis=0)
    ffn_p = np.stack([np.stack([R[2 * b + l // 2]["nffn_p"][l // 2][l % 2] for b in range(B)], axis=0)
                      for l in range(4)], axis=0)
    conv_s = np.concatenate([R[c]["nconv_s"] for c in range(n_cores)], axis=1)
    hgrn_s = np.concatenate([R[c]["nhgrn_s"] for c in range(n_cores)], axis=1)
    ffn_s = np.concatenate([R[c]["nffn_s"] for c in range(n_cores)], axis=1)
    out = (y_prompt, y_sample, conv_p, hgrn_p, ffn_p, conv_s, hgrn_s, ffn_s)
    return tuple(np.ascontiguousarray(o, dtype=np.float32) for o in out), res


def kernel(**inputs):
    inp = {k: np.asarray(v) for k, v in inputs.items()}
    cfg = Cfg()
    out, _ = run(cfg, inp)
    return out
```
